# Optimizing a Trainium2 kernel written in Bass

```python
import math
import jax, jax.numpy as jnp
from jax import lax
import numpy as np


D_MODEL = 1024
BATCH = 8
SEQ = 4096
DEPTH = 4

N_MIXERS = 2
N_SSD = (DEPTH + 1) // 2
N_GMLP = DEPTH // 2

SSD_EXPAND = 2
SSD_INNER = SSD_EXPAND * D_MODEL
SSD_HEADDIM = 64
SSD_HEADS = SSD_INNER // SSD_HEADDIM
SSD_GROUPS = 8
SSD_STATE = 128
SSD_CONV_DIM = SSD_INNER + 2 * SSD_GROUPS * SSD_STATE
SSD_IN_DIM = 2 * SSD_INNER + 2 * SSD_GROUPS * SSD_STATE + SSD_HEADS
CONV_K = 4
CHUNK = 128
DT_MIN = 0.001
DT_MAX = 0.1

GMLP_INNER = 2 * D_MODEL
GMLP_GROUPS = 16
GMLP_GROUP_DIM = GMLP_INNER // GMLP_GROUPS
GMLP_CHUNK = 128

FFN_DIM = int(math.ceil((8 * D_MODEL / 3) / 256) * 256)

PLE_DIM = 256

RMS_EPS = 1e-6
LN_EPS = 1e-5

kernel_name = 'hybrid_ssd_gmlp_trunk'


def rmsnorm(x, w, eps=RMS_EPS):
    xf = x.astype(jnp.float32)
    y = xf * lax.rsqrt(jnp.mean(xf * xf, axis=-1, keepdims=True) + eps)
    return (y * w.astype(jnp.float32)).astype(x.dtype)


def layernorm(x, w, b, eps=LN_EPS):
    xf = x.astype(jnp.float32)
    mu = jnp.mean(xf, axis=-1, keepdims=True)
    xc = xf - mu
    y = xc * lax.rsqrt(jnp.mean(xc * xc, axis=-1, keepdims=True) + eps)
    return (y * w.astype(jnp.float32) + b.astype(jnp.float32)).astype(x.dtype)


def gated_rmsnorm(y, z, w, eps=LN_EPS):
    g = (y * jax.nn.silu(z)).astype(jnp.float32)
    shp = g.shape
    g = g.reshape(shp[:-1] + (SSD_GROUPS, shp[-1] // SSD_GROUPS))
    g = g * lax.rsqrt(jnp.mean(g * g, axis=-1, keepdims=True) + eps)
    return (g.reshape(shp) * w.astype(jnp.float32)).astype(y.dtype)


def causal_dwconv(x, w, b):
    k, c = w.shape
    y = lax.conv_general_dilated(
        x, w[:, None, :].astype(x.dtype), window_strides=(1,), padding=[(k - 1, 0)],
        dimension_numbers=('NWC', 'WIO', 'NWC'), feature_group_count=c)
    return y + b.astype(x.dtype)


def segsum(a):
    t = a.shape[-1]
    cs = jnp.cumsum(a, axis=-1)
    diff = cs[..., :, None] - cs[..., None, :]
    mask = jnp.tril(jnp.ones((t, t), dtype=bool))
    return jnp.where(mask, diff, -jnp.inf)


def ssd_scan(x, dt, a, bm, cm):
    b, s, h, p = x.shape
    g, n = bm.shape[-2], bm.shape[-1]
    r = h // g
    c = s // CHUNK
    dtype = x.dtype
    xr = (x * dt[..., None]).reshape(b, c, CHUNK, g, r, p)
    da = (dt.astype(jnp.float32) * a).reshape(b, c, CHUNK, g, r).transpose(0, 1, 3, 4, 2)
    da_cs = jnp.cumsum(da, axis=-1)
    br = bm.reshape(b, c, CHUNK, g, n)
    cr = cm.reshape(b, c, CHUNK, g, n)
    lmat = jnp.exp(segsum(da)).astype(dtype)
    cb = jnp.einsum('bclgn,bcsgn->bcgls', cr, br)
    wmat = cb[:, :, :, None] * lmat
    y_diag = jnp.einsum('bcgrls,bcsgrp->bclgrp', wmat, xr)
    dec_states = jnp.exp(da_cs[..., -1:] - da_cs).astype(dtype).transpose(0, 1, 4, 2, 3)
    states = jnp.einsum('bclgn,bclgrp->bcgrpn', br, xr * dec_states[..., None])
    chunk_decay = jnp.exp(da_cs[..., -1]).astype(dtype)

    def step(carry, inp):
        dec_c, st_c = inp
        new = dec_c[..., None, None] * carry + st_c
        return new, carry

    init = jnp.zeros((b, g, r, p, n), dtype=states.dtype)
    _, prev = lax.scan(step, init, (jnp.moveaxis(chunk_decay, 1, 0), jnp.moveaxis(states, 1, 0)))
    prev = jnp.moveaxis(prev, 0, 1)
    dec_out = jnp.exp(da_cs).astype(dtype).transpose(0, 1, 4, 2, 3)
    y_off = jnp.einsum('bclgn,bcgrpn->bclgrp', cr, prev) * dec_out[..., None]
    return (y_diag + y_off).reshape(b, s, h, p)


def ssd_mixer(u, w_in, conv_w, conv_b, dt_bias, a_log, d_skip, norm_w, w_out):
    b, s, _ = u.shape
    zxbcdt = u @ w_in
    z = zxbcdt[..., :SSD_INNER]
    xbc = zxbcdt[..., SSD_INNER:SSD_INNER + SSD_CONV_DIM]
    dt = zxbcdt[..., SSD_INNER + SSD_CONV_DIM:]
    xbc = jax.nn.silu(causal_dwconv(xbc, conv_w, conv_b))
    xs = xbc[..., :SSD_INNER].reshape(b, s, SSD_HEADS, SSD_HEADDIM)
    bm = xbc[..., SSD_INNER:SSD_INNER + SSD_GROUPS * SSD_STATE].reshape(b, s, SSD_GROUPS, SSD_STATE)
    cm = xbc[..., SSD_INNER + SSD_GROUPS * SSD_STATE:].reshape(b, s, SSD_GROUPS, SSD_STATE)
    dt = jax.nn.softplus(dt + dt_bias)
    a = -jnp.exp(a_log.astype(jnp.float32))
    y = ssd_scan(xs, dt, a, bm, cm) + xs * d_skip[:, None]
    y = gated_rmsnorm(y.reshape(b, s, SSD_INNER), z, norm_w)
    return y @ w_out


def gmlp_mixer(u, w_in, b_in, ln_w, ln_b, w_s, b_s, w_out):
    b, s, _ = u.shape
    c = s // GMLP_CHUNK
    hp = jax.nn.gelu(u @ w_in + b_in, approximate=False)
    uu = hp[..., :GMLP_INNER]
    vv = layernorm(hp[..., GMLP_INNER:], ln_w, ln_b)
    vv = vv.reshape(b, c, GMLP_CHUNK, GMLP_GROUPS, GMLP_GROUP_DIM)
    ws = jnp.tril(w_s)
    mixed = jnp.einsum('gts,bcsgd->bctgd', ws, vv) + b_s.T[None, None, :, :, None]
    return (uu * mixed.reshape(b, s, GMLP_INNER)) @ w_out


def swiglu(u, w_gate, w_up, w_down):
    return (jax.nn.silu(u @ w_gate) * (u @ w_up)) @ w_down


def setup_inputs(seed: int = 0) -> dict:
    key = jax.random.key(seed)
    ks = jax.random.split(key, 32)
    f32 = jnp.float32

    def nrm(k, shape, scale):
        return jax.random.normal(k, shape, f32) * scale

    x = nrm(ks[0], (BATCH, SEQ, D_MODEL), 1.0)
    p = nrm(ks[1], (DEPTH, BATCH, SEQ, PLE_DIM), 1.0)
    norm_mix = 1.0 + nrm(ks[2], (DEPTH, D_MODEL), 0.02)
    norm_ffn = 1.0 + nrm(ks[3], (DEPTH, D_MODEL), 0.02)
    ssd_w_in = nrm(ks[4], (N_SSD, D_MODEL, SSD_IN_DIM), D_MODEL ** -0.5)
    ssd_conv_w = nrm(ks[5], (N_SSD, CONV_K, SSD_CONV_DIM), CONV_K ** -0.5)
    ssd_conv_b = nrm(ks[6], (N_SSD, SSD_CONV_DIM), 0.02)
    dt0 = jnp.exp(jax.random.uniform(ks[7], (N_SSD, SSD_HEADS), f32)
                  * (math.log(DT_MAX) - math.log(DT_MIN)) + math.log(DT_MIN))
    ssd_dt_bias = dt0 + jnp.log(-jnp.expm1(-dt0))
    ssd_a_log = jnp.log(jax.random.uniform(ks[8], (N_SSD, SSD_HEADS), f32, minval=1.0, maxval=16.0))
    ssd_d = 1.0 + nrm(ks[9], (N_SSD, SSD_HEADS), 0.02)
    ssd_norm_w = 1.0 + nrm(ks[10], (N_SSD, SSD_INNER), 0.02)
    ssd_w_out = nrm(ks[11], (N_SSD, SSD_INNER, D_MODEL), SSD_INNER ** -0.5)
    gmlp_w_in = nrm(ks[12], (N_GMLP, D_MODEL, 2 * GMLP_INNER), D_MODEL ** -0.5)
    gmlp_b_in = nrm(ks[13], (N_GMLP, 2 * GMLP_INNER), 0.02)
    gmlp_ln_w = 1.0 + nrm(ks[14], (N_GMLP, GMLP_INNER), 0.02)
    gmlp_ln_b = nrm(ks[15], (N_GMLP, GMLP_INNER), 0.02)
    gmlp_w_s = nrm(ks[16], (N_GMLP, GMLP_GROUPS, GMLP_CHUNK, GMLP_CHUNK), GMLP_CHUNK ** -0.5)
    gmlp_b_s = 1.0 + nrm(ks[17], (N_GMLP, GMLP_GROUPS, GMLP_CHUNK), 0.02)
    gmlp_w_out = nrm(ks[18], (N_GMLP, GMLP_INNER, D_MODEL), GMLP_INNER ** -0.5)
    ffn_w_gate = nrm(ks[19], (DEPTH, D_MODEL, FFN_DIM), D_MODEL ** -0.5)
    ffn_w_up = nrm(ks[20], (DEPTH, D_MODEL, FFN_DIM), D_MODEL ** -0.5)
    ffn_w_down = nrm(ks[21], (DEPTH, FFN_DIM, D_MODEL), FFN_DIM ** -0.5)
    ple_w_proj = nrm(ks[22], (DEPTH, PLE_DIM, D_MODEL), PLE_DIM ** -0.5)
    ple_norm = 1.0 + nrm(ks[23], (DEPTH, D_MODEL), 0.02)
    ple_gate_norm = 1.0 + nrm(ks[24], (DEPTH, D_MODEL), 0.02)
    ple_w_gate = nrm(ks[25], (DEPTH, D_MODEL, D_MODEL), D_MODEL ** -0.5)
    final_norm = 1.0 + nrm(ks[26], (D_MODEL,), 0.02)
    return {'x': x, 'p': p, 'norm_mix': norm_mix, 'norm_ffn': norm_ffn,
            'ssd_w_in': ssd_w_in, 'ssd_conv_w': ssd_conv_w, 'ssd_conv_b': ssd_conv_b,
            'ssd_dt_bias': ssd_dt_bias, 'ssd_a_log': ssd_a_log, 'ssd_d': ssd_d,
            'ssd_norm_w': ssd_norm_w, 'ssd_w_out': ssd_w_out,
            'gmlp_w_in': gmlp_w_in, 'gmlp_b_in': gmlp_b_in, 'gmlp_ln_w': gmlp_ln_w,
            'gmlp_ln_b': gmlp_ln_b, 'gmlp_w_s': gmlp_w_s, 'gmlp_b_s': gmlp_b_s,
            'gmlp_w_out': gmlp_w_out,
            'ffn_w_gate': ffn_w_gate, 'ffn_w_up': ffn_w_up, 'ffn_w_down': ffn_w_down,
            'ple_w_proj': ple_w_proj, 'ple_norm': ple_norm, 'ple_gate_norm': ple_gate_norm,
            'ple_w_gate': ple_w_gate, 'final_norm': final_norm}


def reference(x, p, norm_mix, norm_ffn,
              ssd_w_in, ssd_conv_w, ssd_conv_b, ssd_dt_bias, ssd_a_log, ssd_d, ssd_norm_w, ssd_w_out,
              gmlp_w_in, gmlp_b_in, gmlp_ln_w, gmlp_ln_b, gmlp_w_s, gmlp_b_s, gmlp_w_out,
              ffn_w_gate, ffn_w_up, ffn_w_down,
              ple_w_proj, ple_norm, ple_gate_norm, ple_w_gate, final_norm):
    h = x
    for i in range(DEPTH):
        j = i // N_MIXERS
        hn = rmsnorm(h, norm_mix[i])
        if i % N_MIXERS == 0:
            mix = ssd_mixer(hn, ssd_w_in[j], ssd_conv_w[j], ssd_conv_b[j], ssd_dt_bias[j],
                            ssd_a_log[j], ssd_d[j], ssd_norm_w[j], ssd_w_out[j])
        else:
            mix = gmlp_mixer(hn, gmlp_w_in[j], gmlp_b_in[j], gmlp_ln_w[j], gmlp_ln_b[j],
                             gmlp_w_s[j], gmlp_b_s[j], gmlp_w_out[j])
        h = h + mix
        h = h + swiglu(rmsnorm(h, norm_ffn[i]), ffn_w_gate[i], ffn_w_up[i], ffn_w_down[i])
        e = rmsnorm(p[i] @ ple_w_proj[i], ple_norm[i])
        gate = jax.nn.sigmoid(rmsnorm(h, ple_gate_norm[i]) @ ple_w_gate[i])
        h = h + gate * e
    return rmsnorm(h, final_norm)
```

```python
import os
import numpy as np
from contextlib import ExitStack
import concourse.bass as bass
import concourse.mybir as mybir
from concourse.bass_utils import run_bass_kernel_spmd

F32 = mybir.dt.float32
BF16 = mybir.dt.bfloat16
AF = mybir.ActivationFunctionType
ALU = mybir.AluOpType

D = 1024
DC = 8
T = 512
NCH = T // 128
DEPTH = 4
SSD_IN = 6176
FFN = 2816
FC = 22
PLE = 256
RMS_EPS = 1e-6
LN_EPS = 1e-5
NW = 5
RS_ENG = os.environ.get("RS_ENG", "dve")
XRD_ENG = os.environ.get("XRD_ENG", "dve")
STM_ENG = os.environ.get("STM_ENG", "dve")
CD_ENG = os.environ.get("CD_ENG", "dve")
NG_SETS = int(os.environ.get("NG_SETS", "2"))
SLOT = 4096

PP_LAYER = 32
PP_FINAL = DEPTH * PP_LAYER
PP_SSD = PP_FINAL + 8
PP_SSD_SZ = 192
PP_GM = PP_SSD + 2 * PP_SSD_SZ
PP_GM_SZ = 16
NPP = PP_GM + 2 * PP_GM_SZ
NRB = 128
NCST = 384


_dbg = {}
_FENCE = {}


class Res:
    __slots__ = ("w", "r", "excl")

    def __init__(self, excl=False):
        self.w = None
        self.r = dict(_FENCE)
        self.excl = excl


class Sched:
    def __init__(self, nc, ndma=40):
        self.nc = nc
        self.eng = {"pe": nc.tensor, "act": nc.scalar, "dve": nc.vector, "pool": nc.gpsimd, "sp": nc.sync}
        self.sem = {k: nc.alloc_semaphore("s_" + k) for k in self.eng}
        self.cnt = {k: 0 for k in self.eng}
        self.seen = {k: {} for k in self.eng}
        self.dsem = [nc.alloc_semaphore("d%d" % i) for i in range(ndma)]
        self.dval = [0] * ndma
        self.drr = 0
        self.nwait = 0
        self.nins = 0
        self.scope_dma = {}
        _FENCE.clear()

    def end_scope(self):
        _FENCE.clear()
        for e in self.eng:
            if self.cnt[e]:
                _FENCE[self.sem[e].num] = (self.sem[e], self.cnt[e])
        for k, ev in self.scope_dma.items():
            _FENCE[k] = ev
        self.scope_dma = {}

    def _wait(self, e, ev):
        sem, val = ev
        if self.seen[e].get(sem.num, 0) >= val:
            return
        self.eng[e].wait_ge(sem, val)
        self.seen[e][sem.num] = val
        self.nwait += 1

    def _deps(self, e, reads, writes):
        deps = {}

        def add(ev):
            if ev is None:
                return
            s, v = ev
            if s.num not in deps or deps[s.num][1] < v:
                deps[s.num] = ev

        for r in reads:
            add(r.w)
            if r.excl:
                for ev in r.r.values():
                    add(ev)
        for w in writes:
            add(w.w)
            for ev in w.r.values():
                add(ev)
        own = self.sem[e].num
        for k, ev in deps.items():
            if k == own and e == "pe":
                continue
            self._wait(e, ev)

    def _record(self, ev, reads, writes):
        for r in reads:
            if r.excl:
                r.w = ev
                r.r = {}
            else:
                r.r[ev[0].num] = ev
        for w in writes:
            w.w = ev
            w.r = {}

    def op(self, e, emit, reads=(), writes=()):
        self._deps(e, reads, writes)
        ins = emit(self.eng[e])
        self.cnt[e] += 1
        ins.then_inc(self.sem[e], 1)
        self.nins += 1
        self._record((self.sem[e], self.cnt[e]), reads, writes)

    def dma(self, q, out, in_, reads=(), writes=(), scoped=False):
        i = self.drr
        self.drr = (i + 1) % len(self.dsem)
        if self.dval[i]:
            self._wait(q, (self.dsem[i], self.dval[i]))
        self._deps(q, reads, writes)
        self.eng[q].dma_start(out=out, in_=in_).then_inc(self.dsem[i], 16)
        self.dval[i] += 16
        self.nins += 1
        ev = (self.dsem[i], self.dval[i])
        if scoped:
            self.scope_dma[ev[0].num] = ev
        self._record(ev, reads, writes)
        return ev


class Buf:
    def __init__(self, t, res=None):
        self.t = t
        self.res = res if res is not None else Res()


def build_program(S_tok, layers=(0, 1, 2, 3), do_final=True, use_scratch=False, parts=("mix", "ffn", "ple")):
    assert S_tok % T == 0
    NT = S_tok // T
    nc = bass.Bass("TRN2", target_bir_lowering=False)
    K = Sched(nc)

    def din(name, shape):
        return nc.dram_tensor(name, list(shape), F32, kind="ExternalInput").ap()

    xT = din("xT", [D, S_tok])
    pT = din("pT", [DEPTH, PLE, S_tok])
    PPd = din("PP", [128, NPP])
    RBd = din("RB", [128, NRB])
    CSTd = din("CST", [128, NCST])
    gvd = din("gv", [2, 3 * 2048])
    wsTd = din("wsT", [2, 128, 2048])
    bsd = din("bs", [2, 2048])
    W = {
        "ssd_w_in": din("ssd_w_in", [2, D, SSD_IN]),
        "ssd_w_out": din("ssd_w_out", [2, 2048, D]),
        "gmlp_w_in": din("gmlp_w_in", [2, D, 4096]),
        "gmlp_w_out": din("gmlp_w_out", [2, 2048, D]),
        "ffn_w_gate": din("ffn_w_gate", [DEPTH, D, FFN]),
        "ffn_w_up": din("ffn_w_up", [DEPTH, D, FFN]),
        "ffn_w_down": din("ffn_w_down", [DEPTH, FFN, D]),
        "ple_w_proj": din("ple_w_proj", [DEPTH, PLE, D]),
        "ple_w_gate": din("ple_w_gate", [DEPTH, D, D]),
    }
    outT = nc.dram_tensor("outT", [D, S_tok], F32, kind="ExternalOutput").ap()

    _uc = [0]

    def un(name):
        _uc[0] += 1
        return "%s_%d" % (name, _uc[0])

    def sb(name, shape, dt):
        return nc.alloc_sbuf_tensor(name, list(shape), dt)

    h = Buf(sb("h", [128, DC, T], F32))
    hn = Buf(sb("hn", [128, DC, T], BF16))
    PP = Buf(sb("PPs", [128, NPP], F32))
    RB = Buf(sb("RBs", [128, NRB], F32))
    aneg = Buf(sb("aneg", [128, 64], F32))
    ident = Buf(sb("ident", [128, 128], BF16))
    tri = Buf(sb("tri", [128, 128], BF16))
    sl = Buf(sb("sl", [128, 128], BF16))
    ones1 = Buf(sb("ones1", [128, 128], BF16))
    onesD = Buf(sb("onesD", [128, 128], BF16))
    ones256 = Buf(sb("ones256", [128, 128], BF16))
    eps_rms = Buf(sb("eps_rms", [128, 1], F32))
    eps_ln = Buf(sb("eps_ln", [128, 1], F32))
    one_c = Buf(sb("one_c", [128, 1], F32))
    n_ssd = sum(1 for i in layers if i % 2 == 0)
    state = {}
    halo = {}
    for i in layers:
        if i % 2 == 0:
            state[i] = Buf(sb("state%d" % i, [128, 32, 64], F32))
            halo[i] = Buf(sb("halo%d" % i, [128, 32, 3], F32))
    slots = [Buf(sb("wslot%d" % i, [128, SLOT], BF16)) for i in range(NW)]
    psum_t = nc.alloc_psum_tensor("ps", [128, 8 * 512], F32)
    banks = [Res(excl=True) for _ in range(8)]
    bank_rr = [0]

    def bank():
        i = bank_rr[0]
        bank_rr[0] = (i + 1) % 8
        return psum_t[:, i * 512:(i + 1) * 512], banks[i]

    def wsrc(name, idx, kc, c0, n):
        return W[name][idx].rearrange("(kc p) n -> p kc n", p=128)[:, :, c0:c0 + n], kc, n

    def layer_plan(i):
        j = i // 2
        pl = []
        if "mix" not in parts:
            pass
        elif i % 2 == 0:
            pl.append(wsrc("ssd_w_in", j, 8, 0, 512))
            pl.append(wsrc("ssd_w_in", j, 8, 6144, 32))
            for b in range(8):
                if 1 <= b < 4:
                    pl.append(wsrc("ssd_w_in", j, 8, b * 512, 512))
                pl.append(wsrc("ssd_w_in", j, 8, 2048 + b * 512, 512))
            for b in range(4):
                pl.append(wsrc("ssd_w_out", j, 16, b * 256, 256))
        else:
            for b in range(8):
                pl.append(wsrc("gmlp_w_in", j, 8, b * 512, 512))
            for b in range(4):
                pl.append(wsrc("gmlp_w_out", j, 16, b * 256, 256))
        if "ple" in parts:
            pl.append(wsrc("ple_w_proj", i, 2, 0, 1024))
        if "ffn" in parts:
            for b in range(6):
                n = 512 if b < 5 else 256
                pl.append(wsrc("ffn_w_gate", i, 8, b * 512, n))
                pl.append(wsrc("ffn_w_up", i, 8, b * 512, n))
            for b in range(8):
                pl.append(wsrc("ffn_w_down", i, FC, b * 128, 128))
        if "ple" in parts:
            for b in range(2):
                pl.append(wsrc("ple_w_gate", i, 8, b * 512, 512))
        return pl

    plan = []
    for tl in range(NT):
        for i in layers:
            for bi, spec in enumerate(layer_plan(i)):
                plan.append(((i, bi), spec, tl))
    scratch = {}
    wst = {"issued": 0, "used": 0}

    def w_issue(idx):
        key, (src, kc, n), tl = plan[idx]
        s = slots[idx % NW]
        view = s.t[:, 0:kc * n].rearrange("p (k n) -> p k n", k=kc)
        if (not use_scratch) or NT == 1:
            K.dma("pool", view, src, writes=[s.res])
        elif tl == 0:
            scr = nc.dram_tensor("scr_%d_%d" % key, [128, kc * n], BF16)
            scratch[key] = (scr, Res())
            K.dma("pool", view, src, writes=[s.res])
            K.dma("sp", scr.ap(), s.t[:, 0:kc * n], reads=[s.res], writes=[scratch[key][1]])
        else:
            scr, sres = scratch[key]
            K.dma("sp", s.t[:, 0:kc * n], scr.ap(), reads=[sres], writes=[s.res])

    def w_next(kc, n):
        idx = wst["used"]
        while wst["issued"] < min(idx + NW - 2, len(plan) - 1) + 1:
            w_issue(wst["issued"])
            wst["issued"] += 1
        key, (src, kc2, n2), tl = plan[idx]
        assert (kc, n) == (kc2, n2), (key, kc, n, kc2, n2)
        wst["used"] += 1
        s = slots[idx % NW]
        return s.t[:, 0:kc * n].rearrange("p (k n) -> p k n", k=kc), s.res

    K.dma("sp", PP.t[:], PPd, writes=[PP.res])
    K.dma("sp", RB.t[:], RBd, writes=[RB.res])
    K.dma("pool", ident.t[:], CSTd[:, 0:128], writes=[ident.res])
    K.dma("pool", tri.t[:], CSTd[:, 128:256], writes=[tri.res])
    K.dma("pool", sl.t[:], CSTd[:, 256:384], writes=[sl.res])
    trif = Buf(sb("trif", [128, 128], F32))
    K.dma("sp", trif.t[:], CSTd[:, 128:256], writes=[trif.res])
    K.op("pool", lambda e: e.memset(ones1.t[:], 1.0), writes=[ones1.res])
    K.op("pool", lambda e: e.memset(onesD.t[:], 1.0 / D), writes=[onesD.res])
    K.op("pool", lambda e: e.memset(ones256.t[:], 1.0 / 256), writes=[ones256.res])
    K.op("pool", lambda e: e.memset(eps_rms.t[:], RMS_EPS), writes=[eps_rms.res])
    K.op("pool", lambda e: e.memset(eps_ln.t[:], LN_EPS), writes=[eps_ln.res])
    K.op("pool", lambda e: e.memset(one_c.t[:], 1.0), writes=[one_c.res])
    for i in state:
        K.op("pool", lambda e, i=i: e.memset(state[i].t[:], 0.0), writes=[state[i].res])
        K.op("pool", lambda e, i=i: e.memset(halo[i].t[:], 0.0), writes=[halo[i].res])
    K.op("act", lambda e: e.activation(out=aneg.t[:].rearrange("p (j h) -> p j h", j=2),
                                       in_=RB.t[:].rearrange("p (j x) -> p j x", j=2)[:, :, 32:64], func=AF.Exp),
         reads=[RB.res], writes=[aneg.res])
    K.op("dve", lambda e: e.tensor_scalar(out=aneg.t[:], in0=aneg.t[:], scalar1=-1.0, scalar2=None, op0=ALU.mult),
         reads=[aneg.res], writes=[aneg.res])

    def mm_group(out_ap, bres, pairs, reads, first=True, last=True):
        def emit(e):
            ins = None
            n = len(pairs)
            for k, (l, r) in enumerate(pairs):
                ins = e.matmul(out_ap, l, r, start=(first and k == 0), stop=(last and k == n - 1))
            return ins
        K.op("pe", emit, reads=reads, writes=[bres])

    def rms_rstd(src, eps, ones, name, scratch_sq, rstd):
        K.op("act", lambda e: e.activation(out=scratch_sq.t[:], in_=src.t[:], func=AF.Square),
             reads=[src.res], writes=[scratch_sq.res])
        pb, pr = bank()
        mm_group(pb, pr, [(ones.t[:], scratch_sq.t[:, c, :]) for c in range(DC)], [ones.res, scratch_sq.res])
        K.op("act", lambda e: e.activation(out=rstd.t[:], in_=pb, func=AF.Ln, bias=eps.t[:]),
             reads=[pr, eps.res], writes=[rstd.res])
        K.op("act", lambda e: e.activation(out=rstd.t[:], in_=rstd.t[:], func=AF.Exp, scale=-0.5),
             reads=[rstd.res], writes=[rstd.res])

    hnc = [Res() for _ in range(DC)]
    hc = [Res() for _ in range(DC)]
    nsq = Buf(sb("nsq", [128, DC, T], BF16))
    nsqc = [Res() for _ in range(DC)]
    nrstd = Buf(sb("nrstd", [128, T], F32))
    lndummy = Buf(sb("lndummy", [128, 1], F32))

    def rms_h():
        K.op("act", lambda e: e.activation(out=lndummy.t[:], in_=one_c.t[:], func=AF.Ln),
             reads=[one_c.res], writes=[lndummy.res])
        pb, pr = bank()
        for c in range(DC):
            K.op("act", lambda e, c=c: e.activation(out=nsq.t[:, c, :], in_=h.t[:, c, :], func=AF.Square),
                 reads=[hc[c]], writes=[nsqc[c]])
            K.op("pe", lambda e, c=c: e.matmul(pb, onesD.t[:], nsq.t[:, c, :], start=(c == 0), stop=(c == DC - 1)),
                 reads=[onesD.res, nsqc[c]], writes=[pr])
        K.op("act", lambda e: e.activation(out=nrstd.t[:], in_=pb, func=AF.Ln, bias=eps_rms.t[:]),
             reads=[pr, eps_rms.res], writes=[nrstd.res])
        K.op("act", lambda e: e.activation(out=nrstd.t[:], in_=nrstd.t[:], func=AF.Exp, scale=-0.5),
             reads=[nrstd.res], writes=[nrstd.res])
        return nrstd

    def normalize(src, rstd, wcol, dst):
        assert dst is hn and src is h
        for c in range(DC):
            K.op("dve", lambda e, c=c: e.scalar_tensor_tensor(
                out=dst.t[:, c, :], in0=src.t[:, c, :], scalar=PP.t[:, wcol + c:wcol + c + 1], in1=rstd.t[:],
                op0=ALU.mult, op1=ALU.mult), reads=[hc[c], PP.res, rstd.res], writes=[dst.res, hnc[c]])

    def mm_kc_outer(w, wres, njj):
        outs = [bank() for _ in range(njj)]
        for kc in range(DC):
            def emit(e, kc=kc):
                ins = None
                for jj in range(njj):
                    ins = e.matmul(outs[jj][0], w[:, kc, jj * 128:(jj + 1) * 128], hn.t[:, kc, :],
                                   start=(kc == 0), stop=(kc == DC - 1))
                return ins
            K.op("pe", emit, reads=[wres, hnc[kc]], writes=[o[1] for o in outs])
        return outs

    def ffn_block(i, mid_hook=None):
        with ExitStack() as es:
            es.callback(K.end_scope)
            sq_t = es.enter_context(nc.sbuf_tensor(un("f_sq"), [128, DC, T], BF16))
            rstd_t = es.enter_context(nc.sbuf_tensor(un("f_rstd"), [128, T], F32))
            act_t = es.enter_context(nc.sbuf_tensor(un("f_act"), [128, FC, T], BF16))
            sg0 = es.enter_context(nc.sbuf_tensor(un("f_sg0"), [128, T], F32))
            sg1 = es.enter_context(nc.sbuf_tensor(un("f_sg1"), [128, T], F32))
            sq = Buf(sq_t)
            rstd = Buf(rstd_t)
            act = [Res() for _ in range(FC)]
            sg = [Buf(sg0), Buf(sg1)]
            normalize(h, rms_h(), i * PP_LAYER + 8, hn)
            if mid_hook is not None:
                mid_hook()
            n = 0
            for b in range(6):
                ncol = 512 if b < 5 else 256
                wg, wgr = w_next(8, ncol)
                wu, wur = w_next(8, ncol)
                pre = mm_kc_outer(wg, wgr, 4) if b == 0 else None
                for jj in range(ncol // 128):
                    c = b * 4 + jj
                    if pre is not None:
                        pg, pgr = pre[jj]
                    else:
                        pg, pgr = bank()
                        mm_group(pg, pgr, [(wg[:, kc, jj * 128:(jj + 1) * 128], hn.t[:, kc, :]) for kc in range(DC)],
                                 [wgr, hn.res])
                    pu, pur = bank()
                    mm_group(pu, pur, [(wu[:, kc, jj * 128:(jj + 1) * 128], hn.t[:, kc, :]) for kc in range(DC)],
                             [wur, hn.res])
                    s = sg[n % 2]
                    n += 1
                    K.op("act", lambda e, s=s, pg=pg: e.activation(out=s.t[:], in_=pg, func=AF.Silu),
                         reads=[pgr], writes=[s.res])
                    K.op("dve", lambda e, s=s, pu=pu, c=c: e.tensor_tensor(out=act_t[:, c, :], in0=s.t[:], in1=pu,
                                                                          op=ALU.mult),
                         reads=[s.res, pur], writes=[act[c]])
            for ob in range(8):
                wd, wdr = w_next(FC, 128)
                po, por = bank()
                mm_group(po, por, [(wd[:, kc, :], act_t[:, kc, :]) for kc in range(FC)], [wdr] + act)
                K.op("dve", lambda e, ob=ob, po=po: e.tensor_tensor(out=h.t[:, ob, :], in0=h.t[:, ob, :], in1=po,
                                                                    op=ALU.add),
                     reads=[por, hc[ob]], writes=[hc[ob]])

    def ple_pre(i, t0, eb):
        with ExitStack() as es:
            es.callback(K.end_scope)
            sq = Buf(es.enter_context(nc.sbuf_tensor(un("p_sq"), [128, DC, T], BF16)))
            rstd_e = Buf(es.enter_context(nc.sbuf_tensor(un("p_rstd_e"), [128, T], F32)))
            pbf = Buf(es.enter_context(nc.sbuf_tensor(un("p_pbf"), [128, 2, T], BF16)))
            K.dma("pool", pbf.t[:], pT[i].rearrange("(kc p) s -> p kc s", p=128)[:, :, t0:t0 + T], writes=[pbf.res], scoped=True)
            wp, wpr = w_next(2, 1024)
            for c in range(DC):
                pb, pr = bank()
                mm_group(pb, pr, [(wp[:, kc, c * 128:(c + 1) * 128], pbf.t[:, kc, :]) for kc in range(2)],
                         [wpr, pbf.res])
                K.op("act", lambda e, c=c, pb=pb: e.activation(out=eb.t[:, c, :], in_=pb, func=AF.Copy),
                     reads=[pr], writes=[eb.res])
            rms_rstd(eb, eps_rms, onesD, "ple_e", sq, rstd_e)
            for c in range(DC):
                K.op("dve", lambda e, c=c: e.scalar_tensor_tensor(
                    out=eb.t[:, c, :], in0=eb.t[:, c, :], scalar=PP.t[:, i * PP_LAYER + 16 + c:i * PP_LAYER + 17 + c],
                    in1=rstd_e.t[:], op0=ALU.mult, op1=ALU.mult), reads=[eb.res, PP.res, rstd_e.res],
                    writes=[eb.res])

    def ple_post(i, eb):
        with ExitStack() as es:
            es.callback(K.end_scope)
            sq = Buf(es.enter_context(nc.sbuf_tensor(un("q_sq"), [128, DC, T], BF16)))
            rstd_g = Buf(es.enter_context(nc.sbuf_tensor(un("q_rstd_g"), [128, T], F32)))
            gt = [Buf(es.enter_context(nc.sbuf_tensor(un("q_g%d" % k), [128, T], F32))) for k in range(2)]
            normalize(h, rms_h(), i * PP_LAYER + 24, hn)
            n = 0
            for b in range(2):
                wg, wgr = w_next(8, 512)
                pre = mm_kc_outer(wg, wgr, 4) if b == 0 else None
                for jj in range(4):
                    c = b * 4 + jj
                    if pre is not None:
                        pg, pgr = pre[jj]
                    else:
                        pg, pgr = bank()
                        mm_group(pg, pgr, [(wg[:, kc, jj * 128:(jj + 1) * 128], hn.t[:, kc, :]) for kc in range(DC)],
                                 [wgr, hn.res])
                    g = gt[n % 2]
                    n += 1
                    K.op("act", lambda e, g=g, pg=pg: e.activation(out=g.t[:], in_=pg, func=AF.Sigmoid),
                         reads=[pgr], writes=[g.res])
                    K.op("dve", lambda e, g=g, c=c: e.tensor_tensor(out=g.t[:], in0=g.t[:], in1=eb.t[:, c, :],
                                                                    op=ALU.mult),
                         reads=[g.res, eb.res], writes=[g.res])
                    K.op("dve", lambda e, g=g, c=c: e.tensor_tensor(out=h.t[:, c, :], in0=h.t[:, c, :], in1=g.t[:],
                                                                    op=ALU.add),
                         reads=[g.res, hc[c]], writes=[hc[c]])

    def gmlp_block(i):
        j = i // 2
        with ExitStack() as es:
            es.callback(K.end_scope)
            sq_t = es.enter_context(nc.sbuf_tensor(un("g_sq"), [128, DC, T], BF16))
            rstd_t = es.enter_context(nc.sbuf_tensor(un("g_rstd"), [128, T], F32))
            uu_t = es.enter_context(nc.sbuf_tensor(un("g_uu"), [128, 16, T], BF16))
            vg_t = es.enter_context(nc.sbuf_tensor(un("g_vg"), [128, NCH, 2048], F32))
            vv0 = es.enter_context(nc.sbuf_tensor(un("g_vv0"), [128, 16, 128], BF16))
            vv1 = es.enter_context(nc.sbuf_tensor(un("g_vv1"), [128, 16, 128], BF16))
            gv_t = es.enter_context(nc.sbuf_tensor(un("g_gv"), [128, 3, 2048], F32))
            wsf_t = es.enter_context(nc.sbuf_tensor(un("g_wsf"), [128, 16, 128], F32))
            ws_t = es.enter_context(nc.sbuf_tensor(un("g_ws"), [128, 16, 128], BF16))
            bs_t = es.enter_context(nc.sbuf_tensor(un("g_bs"), [1, 2048], BF16))
            st_t = es.enter_context(nc.sbuf_tensor(un("g_st"), [128, NCH, 4, 6], F32))
            mv_t = es.enter_context(nc.sbuf_tensor(un("g_mv"), [128, NCH, 2], F32))
            rs_t = es.enter_context(nc.sbuf_tensor(un("g_rs"), [128, NCH], F32))
            sq = Buf(sq_t)
            rstd = Buf(rstd_t)
            uu = [[Res() for _ in range(NCH)] for _ in range(16)]
            vg = [Buf(vg_t[:, tc, :]) for tc in range(NCH)]
            vv = [Buf(vv0), Buf(vv1)]
            gvb = Buf(gv_t)
            wsf = Buf(wsf_t)
            wsb = Buf(ws_t)
            bsb = Buf(bs_t)
            stb = [Buf(st_t[:, tc]) for tc in range(NCH)]
            mvb = [Buf(mv_t[:, tc, :]) for tc in range(NCH)]
            rsb = [Buf(rs_t[:, tc:tc + 1]) for tc in range(NCH)]
            K.dma("sp", gvb.t[:].rearrange("p a n -> p (a n)"), gvd[j:j + 1, :].partition_broadcast(128),
                  writes=[gvb.res], scoped=True)
            K.dma("sp", wsf.t[:].rearrange("p g t -> p (g t)"), wsTd[j], writes=[wsf.res], scoped=True)
            K.dma("pool", bsb.t[:], bsd[j:j + 1, :], writes=[bsb.res], scoped=True)
            K.op("dve", lambda e: e.tensor_tensor(out=wsb.t[:], in0=wsf.t[:],
                                                  in1=trif.t[:].unsqueeze(1).to_broadcast([128, 16, 128]),
                                                  op=ALU.mult), reads=[wsf.res, trif.res], writes=[wsb.res])
            normalize(h, rms_h(), i * PP_LAYER + 0, hn)
            bu = PP_GM + j * PP_GM_SZ
            for b in range(4):
                wu, wur = w_next(8, 512)
                pre = mm_kc_outer(wu, wur, 4) if b == 0 else None
                for jj in range(4):
                    c = b * 4 + jj
                    if pre is not None:
                        pb, pr = pre[jj]
                    else:
                        pb, pr = bank()
                        mm_group(pb, pr, [(wu[:, kc, jj * 128:(jj + 1) * 128], hn.t[:, kc, :]) for kc in range(DC)],
                                 [wur, hn.res])
                    K.op("act", lambda e, c=c, pb=pb: e.activation(out=uu_t[:, c, :], in_=pb, func=AF.Gelu,
                                                                   bias=PP.t[:, bu + c:bu + c + 1]),
                         reads=[pr, PP.res], writes=uu[c])
            pend_bn = None
            for b in range(4):
                wv, wvr = w_next(8, 512)
                for tc in range(NCH):
                    pb, pr = bank()
                    mm_group(pb, pr, [(hn.t[:, kc, tc * 128:(tc + 1) * 128], wv[:, kc, :]) for kc in range(DC)],
                             [wvr, hn.res])
                    dst = vg[tc].t[:, b * 512:(b + 1) * 512]
                    K.op("dve", lambda e, dst=dst, pb=pb, b=b: e.tensor_tensor(
                        out=dst, in0=pb, in1=gvb.t[:, 0, b * 512:(b + 1) * 512], op=ALU.add),
                        reads=[pr, gvb.res], writes=[vg[tc].res])
                    K.op("act", lambda e, dst=dst: e.activation(out=dst, in_=dst, func=AF.Gelu),
                         reads=[vg[tc].res], writes=[vg[tc].res])
                    if pend_bn is not None:
                        pend_bn()

                    def pend_bn(dst=dst, tc=tc, b=b):
                        K.op("dve", lambda e: e.bn_stats(out=stb[tc].t[:, b, :], in_=dst),
                             reads=[vg[tc].res], writes=[stb[tc].res])
            pend_bn()
            pend_gate = None
            for tc in range(NCH):
                K.op("dve", lambda e, tc=tc: e.bn_aggr(out=mvb[tc].t, in_=stb[tc].t.rearrange("p a s -> p (a s)")),
                     reads=[stb[tc].res], writes=[mvb[tc].res])
                K.op("act", lambda e, tc=tc: e.activation(out=rsb[tc].t, in_=mvb[tc].t[:, 1:2], func=AF.Ln,
                                                          bias=eps_ln.t[:]),
                     reads=[mvb[tc].res, eps_ln.res], writes=[rsb[tc].res])
                K.op("act", lambda e, tc=tc: e.activation(out=rsb[tc].t, in_=rsb[tc].t, func=AF.Exp, scale=-0.5),
                     reads=[rsb[tc].res], writes=[rsb[tc].res])
                v = vv[tc % 2]
                K.op("dve", lambda e, tc=tc: e.scalar_tensor_tensor(
                    out=vg[tc].t, in0=vg[tc].t, scalar=mvb[tc].t[:, 0:1], in1=gvb.t[:, 1, :],
                    op0=ALU.subtract, op1=ALU.mult), reads=[vg[tc].res, mvb[tc].res, gvb.res], writes=[vg[tc].res])
                K.op("dve", lambda e, tc=tc, v=v: e.scalar_tensor_tensor(
                    out=v.t[:].rearrange("p g d -> p (g d)"), in0=vg[tc].t, scalar=rsb[tc].t, in1=gvb.t[:, 2, :],
                    op0=ALU.mult, op1=ALU.add), reads=[vg[tc].res, rsb[tc].res, gvb.res], writes=[v.res])
                if pend_gate is not None:
                    pend_gate()
                mixb = []
                for q in range(4):
                    pb, pr = bank()
                    mixb.append((pb, pr))
                    for gg in range(4):
                        g = q * 4 + gg

                        def emit(e, g=g, gg=gg, pb=pb, v=v):
                            o = pb[:, gg * 128:(gg + 1) * 128]
                            e.matmul(o, v.t[:, g, :], wsb.t[:, g, :], start=True, stop=False)
                            return e.matmul(o, ones1.t[0:1, :], bsb.t[0:1, g * 128:(g + 1) * 128], start=False,
                                            stop=True)
                        K.op("pe", emit, reads=[v.res, wsb.res, ones1.res, bsb.res], writes=[pr])

                def pend_gate(tc=tc, mixb=mixb):
                    for q in range(4):
                        pb, pr = mixb[q]
                        K.op("dve", lambda e, q=q, pb=pb: e.tensor_tensor(
                            out=uu_t[:, q * 4:(q + 1) * 4, tc * 128:(tc + 1) * 128],
                            in0=uu_t[:, q * 4:(q + 1) * 4, tc * 128:(tc + 1) * 128],
                            in1=pb.rearrange("p (g t) -> p g t", g=4), op=ALU.mult),
                            reads=[pr] + [uu[q * 4 + gg][tc] for gg in range(4)],
                            writes=[uu[q * 4 + gg][tc] for gg in range(4)])
            pend_gate()
            for ob in range(4):
                wo, wor = w_next(16, 256)
                for oc in range(2):
                    c = ob * 2 + oc
                    pb, pr = bank()
                    mm_group(pb, pr, [(wo[:, kc, oc * 128:(oc + 1) * 128], uu_t[:, kc, :]) for kc in range(16)],
                             [wor] + [uu[kc][tc] for kc in range(16) for tc in range(NCH)])
                    K.op("dve", lambda e, c=c, pb=pb: e.tensor_tensor(out=h.t[:, c, :], in0=h.t[:, c, :], in1=pb,
                                                                      op=ALU.add),
                         reads=[pr, hc[c]], writes=[hc[c]])

    def ssd_block(i):
        j = i // 2
        st = state[i]
        hl = halo[i]
        pbase = PP_SSD + j * PP_SSD_SZ
        cw0, cb0, nw0, dc0 = pbase, pbase + 128, pbase + 160, pbase + 176
        with ExitStack() as es:
            es.callback(K.end_scope)

            def al(name, shape, dt):
                return Buf(es.enter_context(nc.sbuf_tensor(un(name), list(shape), dt)))

            normalize(h, rms_h(), i * PP_LAYER + 0, hn)
            zs_b = al("s_zs", [128, 16, T], BF16)
            xbc_b = al("s_xbc", [128, 32, T], BF16)
            zs_t, xbc_t = zs_b.t, xbc_b.t
            NSTG = 3
            stg = [al("s_stg%d" % k, [128, T + 4], F32) for k in range(NSTG)]
            acc = [al("s_acc%d" % k, [128, T], F32) for k in range(NSTG)]
            dtb = al("s_dt", [128, NCH, 32], F32)
            dab = al("s_da", [128, NCH, 32], F32)
            dahl = al("s_dahl", [128, NCH, 2, 32], BF16)
            dmat = al("s_dmat", [128, 16, 128], BF16)
            xr = [al("s_xr%d" % k, [128, 32, 64], BF16) for k in range(2)]
            xrd = [al("s_xrd%d" % k, [128, 32, 64], BF16) for k in range(2)]
            btm = [al("s_btm%d" % k, [128, 8, 128], BF16) for k in range(2)]
            dsx = [al("s_dsx%d" % k, [128, 64], F32) for k in range(2)]
            stbf = [al("s_stbf%d" % k, [128, 32, 64], BF16) for k in range(2)]
            NG = NG_SETS
            cbm = [al("s_cbm%d" % k, [128, 128], BF16) for k in range(NG)]
            rs = [al("s_rs%d" % k, [128, 2, 4, 128], BF16) for k in range(NG)]
            lt = [al("s_lt%d" % k, [128, 4, 128], BF16) for k in range(NG)]
            db = [al("s_db%d" % k, [128, 4, 128], BF16) for k in range(NG)]
            wt = [al("s_wt%d" % k, [128, 4, 128], BF16) for k in range(NG)]
            cd = [al("s_cd%d" % k, [128, 4, 128], BF16) for k in range(NG)]
            gb = [al("s_g%d" % k, [128, 2, 128], F32) for k in range(NG)]
            gq = [al("s_gq%d" % k, [128, 2, 128], BF16) for k in range(NG)]
            gr = [al("s_gr%d" % k, [128, 128], F32) for k in range(NG)]
            zs = [Res() for _ in range(16)]
            xres = [[Res() for _ in range(NCH)] for _ in range(32)]
            if not _dbg.get("ssd"):
                _dbg["ssd"] = 1
                print("ssd scope: sbuf bytes remaining", nc.sbuf_bytes_remaining)

            for jc in range(16):
                K.op("dve", lambda e, jc=jc: e.tensor_scalar(out=dmat.t[:, jc, :], in0=ident.t[:],
                                                             scalar1=PP.t[:, dc0 + jc:dc0 + jc + 1], scalar2=None,
                                                             op0=ALU.mult),
                     reads=[ident.res, PP.res], writes=[dmat.res])
            K.op("act", lambda e: e.activation(out=stbf[0].t[:], in_=st.t[:], func=AF.Copy),
                 reads=[st.res], writes=[stbf[0].res])
            def z_block(b):
                wz, wzr = w_next(8, 512)
                pre = mm_kc_outer(wz, wzr, 4) if b == 0 else None
                for jj in range(4):
                    c = b * 4 + jj
                    if pre is not None:
                        pb, pr = pre[jj]
                    else:
                        pb, pr = bank()
                        mm_group(pb, pr, [(wz[:, kc, jj * 128:(jj + 1) * 128], hn.t[:, kc, :]) for kc in range(DC)],
                                 [wzr, hn.res])
                    K.op("act", lambda e, c=c, pb=pb: e.activation(out=zs_t[:, c, :], in_=pb, func=AF.Silu),
                         reads=[pr], writes=[zs[c]])
            z_block(0)
            wd, wdr = w_next(8, 32)
            pb, pr = bank()
            for c in range(NCH):
                mm_group(pb[:, c * 32:(c + 1) * 32], pr,
                         [(hn.t[:, kc, c * 128:(c + 1) * 128], wd[:, kc, :]) for kc in range(DC)], [wdr, hn.res])
            K.op("dve", lambda e, pb=pb: e.tensor_tensor(
                out=dtb.t[:], in0=pb[:, 0:NCH * 32].rearrange("p (c h) -> p c h", c=NCH),
                in1=RB.t[:, j * 64:j * 64 + 32].unsqueeze(1).to_broadcast([128, NCH, 32]), op=ALU.add),
                reads=[pr, RB.res], writes=[dtb.res])
            K.op("act", lambda e: e.activation(out=dtb.t[:], in_=dtb.t[:], func=AF.Exp),
                 reads=[dtb.res], writes=[dtb.res])
            K.op("act", lambda e: e.activation(out=dtb.t[:], in_=dtb.t[:], func=AF.Ln, bias=one_c.t[:]),
                 reads=[dtb.res, one_c.res], writes=[dtb.res])
            K.op("dve", lambda e: e.tensor_tensor(
                out=dab.t[:], in0=dtb.t[:], in1=aneg.t[:, j * 32:(j + 1) * 32].unsqueeze(1).to_broadcast([128, NCH, 32]),
                op=ALU.mult), reads=[dtb.res, aneg.res], writes=[dab.res])
            K.op("dve", lambda e: e.tensor_copy(dahl.t[:, :, 0, :], dab.t[:]), reads=[dab.res], writes=[dahl.res])
            K.op("dve", lambda e: e.tensor_tensor(out=dahl.t[:, :, 1, :], in0=dab.t[:], in1=dahl.t[:, :, 0, :],
                                                  op=ALU.subtract), reads=[dab.res, dahl.res], writes=[dahl.res])

            def fixed(k):
                return psum_t[:, k * 512:(k + 1) * 512], banks[k]
            XB = [fixed(0), fixed(1), fixed(2)]
            SB = [fixed(3), fixed(4)]
            CB_ = [fixed(5), fixed(6)]
            MISC = fixed(7)

            def P_x(c, half, bankf):
                tk = slice(c * 128, (c + 1) * 128)
                xrc = xr[c % 2]
                pb, pr = bankf()
                pbb = pb.bitcast(BF16)

                def emit(e):
                    ins = None
                    for jj in range(8):
                        ins = e.transpose(pbb[:, jj * 128:(jj + 1) * 128], xbc_t[:, half * 8 + jj, tk], ident.t[:])
                    return ins
                K.op("pe", emit, reads=[ident.res] + [xres[half * 8 + jj][c] for jj in range(8)], writes=[pr])
                K.op("dve", lambda e: e.tensor_tensor(
                    out=xrc.t[:, half * 16:(half + 1) * 16, :], in0=pbb[:, 0:1024].rearrange("p (h q) -> p h q", h=16),
                    in1=dtb.t[:, c, half * 16:(half + 1) * 16].unsqueeze(2).to_broadcast([128, 16, 64]),
                    op=ALU.mult), reads=[pr, dtb.res], writes=[xrc.res])

            def P_b(c, bankf):
                tk = slice(c * 128, (c + 1) * 128)
                btc = btm[c % 2]
                pb, pr = bankf()
                pbb = pb.bitcast(BF16)

                def emit(e):
                    ins = None
                    for g in range(8):
                        ins = e.transpose(pbb[:, g * 128:(g + 1) * 128], xbc_t[:, 16 + g, tk], ident.t[:])
                    return ins
                K.op("pe", emit, reads=[ident.res] + [xres[16 + g][c] for g in range(8)], writes=[pr])
                K.op("act", lambda e: e.activation(out=btc.t[:].rearrange("p g n -> p (g n)"),
                                                   in_=pbb[:, 0:1024], func=AF.Copy),
                     reads=[pr], writes=[btc.res])

            def P_d(c, bankf):
                xrc, xrdc, dsc = xr[c % 2], xrd[c % 2], dsx[c % 2]
                pb, pr = bankf()
                mm_group(pb[:, 0:32], pr, [(sl.t[:], dahl.t[:, c, 0, :]), (sl.t[:], dahl.t[:, c, 1, :])],
                         [sl.res, dahl.res])
                mm_group(pb[:, 32:64], pr, [(ones1.t[:], dahl.t[:, c, 0, :]), (ones1.t[:], dahl.t[:, c, 1, :])],
                         [ones1.res, dahl.res])
                K.op("act", lambda e: e.activation(out=dsc.t[:], in_=pb[:, 0:64], func=AF.Exp),
                     reads=[pr], writes=[dsc.res])

            def P_r(c, halves=(0, 1)):
                xrc, xrdc, dsc = xr[c % 2], xrd[c % 2], dsx[c % 2]
                for hf in halves:
                    K.op(XRD_ENG, lambda e, hf=hf: e.tensor_tensor(
                        out=xrdc.t[:, hf * 16:(hf + 1) * 16, :], in0=xrc.t[:, hf * 16:(hf + 1) * 16, :],
                        in1=dsc.t[:, hf * 16:(hf + 1) * 16].unsqueeze(2).to_broadcast([128, 16, 64]),
                        op=ALU.mult), reads=[xrc.res, dsc.res], writes=[xrdc.res])

            def U_upd(c, gp, bankf):
                xrdc, btc, dsc = xrd[c % 2], btm[c % 2], dsx[c % 2]
                pb, pr = bankf()
                for gg in range(2):
                    g = gp * 2 + gg
                    mm_group(pb[:, gg * 256:(gg + 1) * 256], pr,
                             [(btc.t[:, g, :], xrdc.t[:, 4 * g:4 * g + 4, :].rearrange("p h q -> p (h q)"))],
                             [btc.res, xrdc.res])
                sv = st.t[:, gp * 8:(gp + 1) * 8, :]
                K.op(STM_ENG, lambda e: e.tensor_tensor(
                    out=sv, in0=sv, in1=dsc.t[:, 32 + gp * 8:32 + (gp + 1) * 8].unsqueeze(2).to_broadcast([128, 8, 64]),
                    op=ALU.mult), reads=[st.res, dsc.res], writes=[st.res])
                K.op("dve", lambda e: e.tensor_tensor(
                    out=sv, in0=sv, in1=pb.rearrange("p (h q) -> p h q", h=8), op=ALU.add),
                    reads=[st.res, pr], writes=[st.res])

            def U_copy(c):
                nb = stbf[(c + 1) % 2]
                K.op("act", lambda e: e.activation(out=nb.t[:], in_=st.t[:], func=AF.Copy),
                     reads=[st.res], writes=[nb.res])

            def G1(c, g):
                tk = slice(c * 128, (c + 1) * 128)
                rs_ = rs[g % NG]
                xb, xbr = XB[g % 3]
                K.op(RS_ENG, lambda e: e.tensor_tensor(
                    out=rs_.t[:], in0=tri.t[:].unsqueeze(1).unsqueeze(1).to_broadcast([128, 2, 4, 128]),
                    in1=dahl.t[:, c, :, g * 4:(g + 1) * 4].unsqueeze(3).to_broadcast([128, 2, 4, 128]),
                    op=ALU.mult), reads=[tri.res, dahl.res], writes=[rs_.res])
                mm_group(xb[:, 0:128], xbr, [(xbc_t[:, 16 + g, tk], xbc_t[:, 24 + g, tk])],
                         [xres[16 + g][c], xres[24 + g][c]])
                pseg, psegr = SB[g % 2]
                mm_group(pseg, psegr, [(sl.t[:], rs_.t[:, 0].rearrange("p h l -> p (h l)")),
                                       (sl.t[:], rs_.t[:, 1].rearrange("p h l -> p (h l)"))], [sl.res, rs_.res])
                pcs, pcsr = CB_[g % 2]
                mm_group(pcs, pcsr, [(ones1.t[:], rs_.t[:, 0].rearrange("p h l -> p (h l)")),
                                     (ones1.t[:], rs_.t[:, 1].rearrange("p h l -> p (h l)"))], [ones1.res, rs_.res])

            def G2(c, g):
                tk = slice(c * 128, (c + 1) * 128)
                k3 = g % NG
                xb, xbr = XB[g % 3]
                pseg, psegr = SB[g % 2]
                pcs, pcsr = CB_[g % 2]
                Cfm = xbc_t[:, 24 + g, tk]
                K.op("dve", lambda e: e.tensor_tensor(out=cbm[k3].t[:], in0=xb[:, 0:128], in1=trif.t[:], op=ALU.mult),
                     reads=[xbr, trif.res], writes=[cbm[k3].res])
                K.op("act", lambda e: e.activation(out=lt[k3].t[:].rearrange("p h l -> p (h l)"), in_=pseg, func=AF.Exp),
                     reads=[psegr], writes=[lt[k3].res])
                K.op("act", lambda e: e.activation(out=db[k3].t[:].rearrange("p h l -> p (h l)"), in_=pcs, func=AF.Exp),
                     reads=[pcsr], writes=[db[k3].res])
                K.op("dve", lambda e: e.tensor_tensor(
                    out=wt[k3].t[:], in0=lt[k3].t[:], in1=cbm[k3].t[:].unsqueeze(1).to_broadcast([128, 4, 128]),
                    op=ALU.mult), reads=[lt[k3].res, cbm[k3].res], writes=[wt[k3].res])
                K.op(CD_ENG, lambda e: e.tensor_tensor(
                    out=cd[k3].t[:], in0=db[k3].t[:], in1=Cfm.unsqueeze(1).to_broadcast([128, 4, 128]),
                    op=ALU.mult), reads=[db[k3].res, xres[24 + g][c]], writes=[cd[k3].res])

            def G3(c, g):
                tk = slice(c * 128, (c + 1) * 128)
                k3 = g % NG
                xb, xbr = XB[g % 3]
                xrc = xr[c % 2]
                sbf = stbf[c % 2]
                py = xb[:, 128:384]

                def emit(e):
                    ins = None
                    for jj in range(2):
                        jc = 2 * g + jj
                        o = py[:, jj * 128:(jj + 1) * 128]
                        e.matmul(o, dmat.t[:, jc, :], xbc_t[:, jc, tk], start=True, stop=False)
                        for hh in range(2):
                            hp = 2 * jj + hh
                            hd = 4 * g + hp
                            oh = py[hh * 64:(hh + 1) * 64, jj * 128:(jj + 1) * 128]
                            e.matmul(oh, xrc.t[:, hd, :], wt[k3].t[:, hp, :], start=False, stop=False)
                            ins = e.matmul(oh, sbf.t[:, hd, :], cd[k3].t[:, hp, :], start=False, stop=(hh == 1))
                    return ins
                K.op("pe", emit, reads=[dmat.res, xres[2 * g][c], xres[2 * g + 1][c], xrc.res, wt[k3].res, sbf.res,
                                        cd[k3].res], writes=[xbr])

            def G4a(c, g):
                tk = slice(c * 128, (c + 1) * 128)
                k3 = g % NG
                xb, xbr = XB[g % 3]
                K.op("dve", lambda e: e.tensor_tensor(
                    out=gb[k3].t[:], in0=xb[:, 128:384].rearrange("p (a l) -> p a l", a=2),
                    in1=zs_t[:, 2 * g:2 * g + 2, tk], op=ALU.mult),
                    reads=[xbr, zs[2 * g], zs[2 * g + 1]], writes=[gb[k3].res])
                K.op("act", lambda e: e.activation(out=gq[k3].t[:], in_=gb[k3].t[:], func=AF.Square),
                     reads=[gb[k3].res], writes=[gq[k3].res])

            def G4b1(c, g):
                k3 = g % NG
                xb, xbr = XB[g % 3]
                pss = xb[:, 384:512]
                mm_group(pss, xbr, [(ones256.t[:], gq[k3].t[:, 0, :]), (ones256.t[:], gq[k3].t[:, 1, :])],
                         [ones256.res, gq[k3].res])
                K.op("act", lambda e: e.activation(out=gr[k3].t[:], in_=pss, func=AF.Ln, bias=eps_ln.t[:]),
                     reads=[xbr, eps_ln.res], writes=[gr[k3].res])
                K.op("act", lambda e: e.activation(out=gr[k3].t[:], in_=gr[k3].t[:], func=AF.Exp, scale=-0.5),
                     reads=[gr[k3].res], writes=[gr[k3].res])

            def G4b2(c, g):
                tk = slice(c * 128, (c + 1) * 128)
                k3 = g % NG
                for jj in range(2):
                    jc = 2 * g + jj
                    K.op("dve", lambda e, jj=jj, jc=jc: e.scalar_tensor_tensor(
                        out=xbc_t[:, jc, tk], in0=gb[k3].t[:, jj, :], scalar=PP.t[:, nw0 + jc:nw0 + jc + 1],
                        in1=gr[k3].t[:], op0=ALU.mult, op1=ALU.mult),
                        reads=[gb[k3].res, PP.res, gr[k3].res], writes=[xres[jc][c]])

            def misc_bank():
                return MISC

            P_d(0, bank)

            def hoist(done):
                if done == 7:
                    P_x(0, 0, bank)
                elif done == 15:
                    P_x(0, 1, bank)
                    P_r(0)
                elif done == 23:
                    P_b(0, bank)
                    for gp in range(4):
                        U_upd(0, gp, bank)
            n = 0
            pending = None
            for b in range(8):
                if 1 <= b < 4:
                    z_block(b)
                wx, wxr = w_next(8, 512)
                for jj in range(4):
                    ch = b * 4 + jj
                    pb, pr = bank()
                    mm_group(pb, pr, [(wx[:, kc, jj * 128:(jj + 1) * 128], hn.t[:, kc, :]) for kc in range(DC)],
                             [wxr, hn.res])
                    sg = stg[n % NSTG]
                    ac = acc[n % NSTG]
                    n += 1
                    K.op("act", lambda e, sg=sg, ch=ch: e.activation(out=sg.t[:, 0:3], in_=hl.t[:, ch, :], func=AF.Copy),
                         reads=[hl.res], writes=[sg.res])
                    K.op("act", lambda e, sg=sg, pb=pb: e.activation(out=sg.t[:, 3:3 + T], in_=pb, func=AF.Copy),
                         reads=[pr], writes=[sg.res])
                    K.op("act", lambda e, ac=ac, pb=pb, ch=ch: e.activation(
                        out=ac.t[:], in_=pb, func=AF.Copy, scale=PP.t[:, cw0 + ch * 4 + 3:cw0 + ch * 4 + 4]),
                        reads=[pr, PP.res], writes=[ac.res])
                    K.op("act", lambda e, sg=sg, ch=ch: e.activation(out=hl.t[:, ch, :], in_=sg.t[:, T:T + 3], func=AF.Copy),
                         reads=[sg.res], writes=[hl.res])
                    if pending is not None:
                        pending()
                        hoist(n - 2)
                    for k in range(0, 3):
                        K.op("dve", lambda e, sg=sg, ac=ac, ch=ch, k=k: e.scalar_tensor_tensor(
                            out=ac.t[:], in0=sg.t[:, k:k + T], scalar=PP.t[:, cw0 + ch * 4 + k:cw0 + ch * 4 + k + 1],
                            in1=ac.t[:], op0=ALU.mult, op1=ALU.add), reads=[sg.res, PP.res, ac.res], writes=[ac.res])

                    def pending(ac=ac, ch=ch):
                        K.op("act", lambda e: e.activation(out=xbc_t[:, ch, :], in_=ac.t[:], func=AF.Silu,
                                                           bias=PP.t[:, cb0 + ch:cb0 + ch + 1]),
                             reads=[ac.res, PP.res], writes=xres[ch])
            pending()
            hoist(31)
            NGRP = NCH * 8

            def cg(G):
                return G // 8, G % 8
            for t in range(NGRP + 3):
                if 0 <= t - 3 < NGRP:
                    G4b1(*cg(t - 3))
                if t < NGRP:
                    G1(*cg(t))
                if 0 <= t - 1 < NGRP:
                    G2(*cg(t - 1))
                    G3(*cg(t - 1))
                if 0 <= t - 2 < NGRP:
                    G4a(*cg(t - 2))
                if 0 <= t - 3 < NGRP:
                    G4b2(*cg(t - 3))
                c, r = t // 8, t % 8
                if c < NCH:
                    if r == 0:
                        if c >= 1:
                            U_upd(c, 3, misc_bank)
                        U_copy(c)
                    if c + 1 < NCH:
                        if r == 1:
                            P_d(c + 1, misc_bank)
                        elif r == 2:
                            P_x(c + 1, 0, misc_bank)
                        elif r == 3:
                            P_x(c + 1, 1, misc_bank)
                            P_b(c + 1, misc_bank)
                        elif r == 4:
                            P_r(c + 1, (0,))
                        elif r == 5:
                            P_r(c + 1, (1,))
                            U_upd(c + 1, 0, misc_bank)
                        elif r == 6:
                            U_upd(c + 1, 1, misc_bank)
                        elif r == 7:
                            U_upd(c + 1, 2, misc_bank)
            allx = [xres[kc][tc] for kc in range(16) for tc in range(NCH)]
            for ob in range(4):
                wo, wor = w_next(16, 256)
                for oc in range(2):
                    cc = ob * 2 + oc
                    pb, pr = bank()
                    mm_group(pb, pr, [(wo[:, kc, oc * 128:(oc + 1) * 128], xbc_t[:, kc, :]) for kc in range(16)],
                             [wor] + allx)
                    K.op("dve", lambda e, cc=cc, pb=pb: e.tensor_tensor(out=h.t[:, cc, :], in0=h.t[:, cc, :], in1=pb,
                                                                        op=ALU.add),
                         reads=[pr, hc[cc]], writes=[hc[cc]])

    out_events = []
    for tl in range(NT):
        t0 = tl * T
        for c in range(DC):
            K.dma("sp", h.t[:, c, :], xT[c * 128:(c + 1) * 128, t0:t0 + T], writes=[hc[c]])
        for i in layers:
            if "mix" in parts:
                if i % 2 == 0:
                    ssd_block(i)
                else:
                    gmlp_block(i)
            with ExitStack() as es_l:
                es_l.callback(K.end_scope)
                eb = None
                if "ple" in parts:
                    eb = Buf(es_l.enter_context(nc.sbuf_tensor(un("l_e"), [128, DC, T], F32)))
                if "ffn" in parts:
                    ffn_block(i, (lambda i=i, t0=t0, eb=eb: ple_pre(i, t0, eb)) if eb is not None else None)
                elif eb is not None:
                    ple_pre(i, t0, eb)
                if eb is not None:
                    ple_post(i, eb)
        if do_final:
            with ExitStack() as es:
                es.callback(K.end_scope)
                sq_t = es.enter_context(nc.sbuf_tensor(un("o_sq"), [128, DC, T], BF16))
                rstd_t = es.enter_context(nc.sbuf_tensor(un("o_rstd"), [128, T], F32))
                o_t = es.enter_context(nc.sbuf_tensor(un("o_out"), [128, DC, T], F32))
                sq = Buf(sq_t)
                rstd = Buf(rstd_t)
                ob = Buf(o_t)
                rstd = rms_h()
                for c in range(DC):
                    K.op("dve", lambda e, c=c: e.scalar_tensor_tensor(
                        out=ob.t[:, c, :], in0=h.t[:, c, :], scalar=PP.t[:, PP_FINAL + c:PP_FINAL + c + 1],
                        in1=rstd.t[:], op0=ALU.mult, op1=ALU.mult), reads=[hc[c], PP.res, rstd.res], writes=[ob.res])
                last = K.dma("sp", outT.rearrange("(c p) s -> p c s", p=128)[:, :, t0:t0 + T], ob.t[:],
                             reads=[ob.res], scoped=True)
                out_events.append(last)
        else:
            last = K.dma("sp", outT.rearrange("(c p) s -> p c s", p=128)[:, :, t0:t0 + T], h.t[:], reads=hc)
            out_events.append(last)
    for ev in out_events:
        K.seen["sp"].pop(ev[0].num, None)
        K._wait("sp", ev)
    print("program: %d instructions, %d waits" % (K.nins, K.nwait))
    return nc


def _cols(v):
    v = np.asarray(v, np.float32)
    return np.ascontiguousarray(v.reshape(-1, 128).T)


def pack_params(inp):
    PPh = np.zeros((128, NPP), np.float32)
    for i in range(DEPTH):
        b = i * PP_LAYER
        PPh[:, b:b + 8] = _cols(inp["norm_mix"][i])
        PPh[:, b + 8:b + 16] = _cols(inp["norm_ffn"][i])
        PPh[:, b + 16:b + 24] = _cols(inp["ple_norm"][i])
        PPh[:, b + 24:b + 32] = _cols(inp["ple_gate_norm"][i])
    PPh[:, PP_FINAL:PP_FINAL + 8] = _cols(inp["final_norm"])
    for j in range(2):
        b = PP_SSD + j * PP_SSD_SZ
        cw = np.asarray(inp["ssd_conv_w"][j], np.float32)
        PPh[:, b:b + 128] = cw.reshape(4, 32, 128).transpose(2, 1, 0).reshape(128, 128)
        PPh[:, b + 128:b + 160] = _cols(inp["ssd_conv_b"][j])
        PPh[:, b + 160:b + 176] = _cols(inp["ssd_norm_w"][j])
        dfull = np.repeat(np.asarray(inp["ssd_d"][j], np.float32), 64)
        PPh[:, b + 176:b + 192] = _cols(dfull)
        b2 = PP_GM + j * PP_GM_SZ
        PPh[:, b2:b2 + 16] = _cols(np.asarray(inp["gmlp_b_in"][j])[:2048])
    RBh = np.zeros((128, NRB), np.float32)
    for j in range(2):
        RBh[:, j * 64:j * 64 + 32] = np.asarray(inp["ssd_dt_bias"][j], np.float32)[None, :]
        RBh[:, j * 64 + 32:j * 64 + 64] = np.asarray(inp["ssd_a_log"][j], np.float32)[None, :]
    CSTh = np.zeros((128, NCST), np.float32)
    CSTh[:, 0:128] = np.eye(128, dtype=np.float32)
    k = np.arange(128)
    CSTh[:, 128:256] = (k[:, None] <= k[None, :]).astype(np.float32)
    CSTh[:, 256:384] = (k[:, None] > k[None, :]).astype(np.float32)
    gv = np.zeros((2, 3 * 2048), np.float32)
    wsT = np.zeros((2, 128, 2048), np.float32)
    bs = np.zeros((2, 2048), np.float32)
    for j in range(2):
        gv[j, 0:2048] = np.asarray(inp["gmlp_b_in"][j])[2048:]
        gv[j, 2048:4096] = inp["gmlp_ln_w"][j]
        gv[j, 4096:6144] = inp["gmlp_ln_b"][j]
        wsT[j] = np.asarray(inp["gmlp_w_s"][j], np.float32).transpose(2, 0, 1).reshape(128, 2048)
        bs[j] = np.asarray(inp["gmlp_b_s"][j], np.float32).reshape(2048)
    return {"PP": PPh, "RB": RBh, "CST": CSTh, "gv": gv, "wsT": wsT, "bs": bs}


_WNAMES = ["ssd_w_in", "ssd_w_out", "gmlp_w_in", "gmlp_w_out", "ffn_w_gate", "ffn_w_up", "ffn_w_down",
           "ple_w_proj", "ple_w_gate"]


def make_in_maps(inp, n_cores, S_tok):
    shared = pack_params(inp)
    for n in _WNAMES:
        shared[n] = np.ascontiguousarray(np.asarray(inp[n], np.float32))
    x = np.asarray(inp["x"], np.float32)
    p = np.asarray(inp["p"], np.float32)
    maps = []
    for b in range(n_cores):
        m = dict(shared)
        m["xT"] = np.ascontiguousarray(x[b, :S_tok].T)
        m["pT"] = np.ascontiguousarray(p[:, b, :S_tok].transpose(0, 2, 1))
        maps.append(m)
    return maps


def kernel(**inputs):
    x = np.asarray(inputs["x"])
    B, S_tok, _ = x.shape
    nc = build_program(S_tok)
    maps = make_in_maps(inputs, B, S_tok)
    res = run_bass_kernel_spmd(nc, maps, core_ids=list(range(B)))
    out = np.stack([np.ascontiguousarray(res.results[b]["outT"].T) for b in range(B)], axis=0)
    return out.astype(np.float32)
```

```python
import os
import numpy as np
from contextlib import ExitStack
import concourse.bass as bass
import concourse.mybir as mybir
from concourse.bass_utils import run_bass_kernel_spmd

F32 = mybir.dt.float32
BF16 = mybir.dt.bfloat16
AF = mybir.ActivationFunctionType
ALU = mybir.AluOpType

D = 1024
DC = 8
T = 512
NCH = T // 128
DEPTH = 4
SSD_IN = 6176
FFN = 2816
FC = 22
PLE = 256
RMS_EPS = 1e-6
LN_EPS = 1e-5
NW = 4
RS_ENG = os.environ.get("RS_ENG", "dve")
XRD_ENG = os.environ.get("XRD_ENG", "dve")
STM_ENG = os.environ.get("STM_ENG", "dve")
CD_ENG = os.environ.get("CD_ENG", "dve")
NG_SETS = int(os.environ.get("NG_SETS", "2"))
SLOT = 4096

PP_LAYER = 32
PP_FINAL = DEPTH * PP_LAYER
PP_SSD = PP_FINAL + 8
PP_SSD_SZ = 192
PP_GM = PP_SSD + 2 * PP_SSD_SZ
PP_GM_SZ = 16
NPP = PP_GM + 2 * PP_GM_SZ
NRB = 128
NCST = 384


_dbg = {}
_FENCE = {}


class Res:
    __slots__ = ("w", "r", "excl")

    def __init__(self, excl=False):
        self.w = None
        self.r = dict(_FENCE)
        self.excl = excl


class Sched:
    def __init__(self, nc, ndma=40):
        self.nc = nc
        self.eng = {"pe": nc.tensor, "act": nc.scalar, "dve": nc.vector, "pool": nc.gpsimd, "sp": nc.sync}
        self.sem = {k: nc.alloc_semaphore("s_" + k) for k in self.eng}
        self.cnt = {k: 0 for k in self.eng}
        self.seen = {k: {} for k in self.eng}
        self.dsem = [nc.alloc_semaphore("d%d" % i) for i in range(ndma)]
        self.dval = [0] * ndma
        self.drr = 0
        self.nwait = 0
        self.nins = 0
        self.scope_dma = {}
        _FENCE.clear()

    def end_scope(self):
        _FENCE.clear()
        for e in self.eng:
            if self.cnt[e]:
                _FENCE[self.sem[e].num] = (self.sem[e], self.cnt[e])
        for k, ev in self.scope_dma.items():
            _FENCE[k] = ev
        self.scope_dma = {}

    def _wait(self, e, ev):
        sem, val = ev
        if self.seen[e].get(sem.num, 0) >= val:
            return
        self.eng[e].wait_ge(sem, val)
        self.seen[e][sem.num] = val
        self.nwait += 1

    def _deps(self, e, reads, writes):
        deps = {}

        def add(ev):
            if ev is None:
                return
            s, v = ev
            if s.num not in deps or deps[s.num][1] < v:
                deps[s.num] = ev

        for r in reads:
            add(r.w)
            if r.excl:
                for ev in r.r.values():
                    add(ev)
        for w in writes:
            add(w.w)
            for ev in w.r.values():
                add(ev)
        own = self.sem[e].num
        for k, ev in deps.items():
            if k == own and e == "pe":
                continue
            self._wait(e, ev)

    def _record(self, ev, reads, writes):
        for r in reads:
            if r.excl:
                r.w = ev
                r.r = {}
            else:
                r.r[ev[0].num] = ev
        for w in writes:
            w.w = ev
            w.r = {}

    def op(self, e, emit, reads=(), writes=()):
        self._deps(e, reads, writes)
        ins = emit(self.eng[e])
        self.cnt[e] += 1
        ins.then_inc(self.sem[e], 1)
        self.nins += 1
        self._record((self.sem[e], self.cnt[e]), reads, writes)

    def dma(self, q, out, in_, reads=(), writes=(), scoped=False):
        i = self.drr
        self.drr = (i + 1) % len(self.dsem)
        if self.dval[i]:
            self._wait(q, (self.dsem[i], self.dval[i]))
        self._deps(q, reads, writes)
        self.eng[q].dma_start(out=out, in_=in_).then_inc(self.dsem[i], 16)
        self.dval[i] += 16
        self.nins += 1
        ev = (self.dsem[i], self.dval[i])
        if scoped:
            self.scope_dma[ev[0].num] = ev
        self._record(ev, reads, writes)
        return ev


class Buf:
    def __init__(self, t, res=None):
        self.t = t
        self.res = res if res is not None else Res()


def build_program(S_tok, layers=(0, 1, 2, 3), do_final=True, use_scratch=False, parts=("mix", "ffn", "ple")):
    assert S_tok % T == 0
    NT = S_tok // T
    nc = bass.Bass("TRN2", target_bir_lowering=False)
    K = Sched(nc)

    def din(name, shape):
        return nc.dram_tensor(name, list(shape), F32, kind="ExternalInput").ap()

    xT = din("xT", [D, S_tok])
    pT = din("pT", [DEPTH, PLE, S_tok])
    PPd = din("PP", [128, NPP])
    RBd = din("RB", [128, NRB])
    CSTd = din("CST", [128, NCST])
    gvd = din("gv", [2, 3 * 2048])
    wsTd = din("wsT", [2, 128, 2048])
    bsd = din("bs", [2, 2048])
    W = {
        "ssd_w_in": din("ssd_w_in", [2, D, SSD_IN]),
        "ssd_w_out": din("ssd_w_out", [2, 2048, D]),
        "gmlp_w_in": din("gmlp_w_in", [2, D, 4096]),
        "gmlp_w_out": din("gmlp_w_out", [2, 2048, D]),
        "ffn_w_gate": din("ffn_w_gate", [DEPTH, D, FFN]),
        "ffn_w_up": din("ffn_w_up", [DEPTH, D, FFN]),
        "ffn_w_down": din("ffn_w_down", [DEPTH, FFN, D]),
        "ple_w_proj": din("ple_w_proj", [DEPTH, PLE, D]),
        "ple_w_gate": din("ple_w_gate", [DEPTH, D, D]),
    }
    outT = nc.dram_tensor("outT", [D, S_tok], F32, kind="ExternalOutput").ap()

    _uc = [0]

    def un(name):
        _uc[0] += 1
        return "%s_%d" % (name, _uc[0])

    def sb(name, shape, dt):
        return nc.alloc_sbuf_tensor(name, list(shape), dt)

    h = Buf(sb("h", [128, DC, T], F32))
    hn = Buf(sb("hn", [128, DC, T], BF16))
    PP = Buf(sb("PPs", [128, NPP], F32))
    RB = Buf(sb("RBs", [128, NRB], F32))
    aneg = Buf(sb("aneg", [128, 64], F32))
    ident = Buf(sb("ident", [128, 128], BF16))
    tri = Buf(sb("tri", [128, 128], BF16))
    sl = Buf(sb("sl", [128, 128], BF16))
    ones1 = Buf(sb("ones1", [128, 128], BF16))
    onesD = Buf(sb("onesD", [128, 128], BF16))
    ones256 = Buf(sb("ones256", [128, 128], BF16))
    eps_rms = Buf(sb("eps_rms", [128, 1], F32))
    eps_ln = Buf(sb("eps_ln", [128, 1], F32))
    one_c = Buf(sb("one_c", [128, 1], F32))
    n_ssd = sum(1 for i in layers if i % 2 == 0)
    state = {}
    halo = {}
    for i in layers:
        if i % 2 == 0:
            state[i] = Buf(sb("state%d" % i, [128, 32, 64], F32))
            halo[i] = Buf(sb("halo%d" % i, [128, 32, 3], F32))
    slots = [Buf(sb("wslot%d" % i, [128, SLOT], BF16)) for i in range(NW)]
    psum_t = nc.alloc_psum_tensor("ps", [128, 8 * 512], F32)
    banks = [Res(excl=True) for _ in range(8)]
    bank_rr = [0]

    def bank():
        i = bank_rr[0]
        bank_rr[0] = (i + 1) % 8
        return psum_t[:, i * 512:(i + 1) * 512], banks[i]

    def wsrc(name, idx, kc, c0, n):
        return W[name][idx].rearrange("(kc p) n -> p kc n", p=128)[:, :, c0:c0 + n], kc, n

    def layer_plan(i):
        j = i // 2
        pl = []
        if "mix" not in parts:
            pass
        elif i % 2 == 0:
            pl.append(wsrc("ssd_w_in", j, 8, 0, 512))
            pl.append(wsrc("ssd_w_in", j, 8, 6144, 32))
            for b in range(8):
                if 1 <= b < 4:
                    pl.append(wsrc("ssd_w_in", j, 8, b * 512, 512))
                pl.append(wsrc("ssd_w_in", j, 8, 2048 + b * 512, 512))
            for b in range(4):
                pl.append(wsrc("ssd_w_out", j, 16, b * 256, 256))
        else:
            for b in range(8):
                pl.append(wsrc("gmlp_w_in", j, 8, b * 512, 512))
            for b in range(4):
                pl.append(wsrc("gmlp_w_out", j, 16, b * 256, 256))
        if "ple" in parts:
            pl.append(wsrc("ple_w_proj", i, 2, 0, 1024))
        if "ffn" in parts:
            for b in range(6):
                n = 512 if b < 5 else 256
                pl.append(wsrc("ffn_w_gate", i, 8, b * 512, n))
                pl.append(wsrc("ffn_w_up", i, 8, b * 512, n))
            for b in range(8):
                pl.append(wsrc("ffn_w_down", i, FC, b * 128, 128))
        if "ple" in parts:
            for b in range(2):
                pl.append(wsrc("ple_w_gate", i, 8, b * 512, 512))
        return pl

    plan = []
    for tl in range(NT):
        for i in layers:
            for bi, spec in enumerate(layer_plan(i)):
                plan.append(((i, bi), spec, tl))
    scratch = {}
    wst = {"issued": 0, "used": 0}

    def w_issue(idx):
        key, (src, kc, n), tl = plan[idx]
        s = slots[idx % NW]
        view = s.t[:, 0:kc * n].rearrange("p (k n) -> p k n", k=kc)
        if (not use_scratch) or NT == 1:
            K.dma("pool", view, src, writes=[s.res])
        elif tl == 0:
            scr = nc.dram_tensor("scr_%d_%d" % key, [128, kc * n], BF16)
            scratch[key] = (scr, Res())
            K.dma("pool", view, src, writes=[s.res])
            K.dma("sp", scr.ap(), s.t[:, 0:kc * n], reads=[s.res], writes=[scratch[key][1]])
        else:
            scr, sres = scratch[key]
            K.dma("sp", s.t[:, 0:kc * n], scr.ap(), reads=[sres], writes=[s.res])

    def w_next(kc, n):
        idx = wst["used"]
        while wst["issued"] < min(idx + NW - 2, len(plan) - 1) + 1:
            w_issue(wst["issued"])
            wst["issued"] += 1
        key, (src, kc2, n2), tl = plan[idx]
        assert (kc, n) == (kc2, n2), (key, kc, n, kc2, n2)
        wst["used"] += 1
        s = slots[idx % NW]
        return s.t[:, 0:kc * n].rearrange("p (k n) -> p k n", k=kc), s.res

    K.dma("sp", PP.t[:], PPd, writes=[PP.res])
    K.dma("sp", RB.t[:], RBd, writes=[RB.res])
    K.dma("pool", ident.t[:], CSTd[:, 0:128], writes=[ident.res])
    K.dma("pool", tri.t[:], CSTd[:, 128:256], writes=[tri.res])
    K.dma("pool", sl.t[:], CSTd[:, 256:384], writes=[sl.res])
    trif = Buf(sb("trif", [128, 128], F32))
    K.dma("sp", trif.t[:], CSTd[:, 128:256], writes=[trif.res])
    K.op("pool", lambda e: e.memset(ones1.t[:], 1.0), writes=[ones1.res])
    K.op("pool", lambda e: e.memset(onesD.t[:], 1.0 / D), writes=[onesD.res])
    K.op("pool", lambda e: e.memset(ones256.t[:], 1.0 / 256), writes=[ones256.res])
    K.op("pool", lambda e: e.memset(eps_rms.t[:], RMS_EPS), writes=[eps_rms.res])
    K.op("pool", lambda e: e.memset(eps_ln.t[:], LN_EPS), writes=[eps_ln.res])
    K.op("pool", lambda e: e.memset(one_c.t[:], 1.0), writes=[one_c.res])
    for i in state:
        K.op("pool", lambda e, i=i: e.memset(state[i].t[:], 0.0), writes=[state[i].res])
        K.op("pool", lambda e, i=i: e.memset(halo[i].t[:], 0.0), writes=[halo[i].res])
    K.op("act", lambda e: e.activation(out=aneg.t[:].rearrange("p (j h) -> p j h", j=2),
                                       in_=RB.t[:].rearrange("p (j x) -> p j x", j=2)[:, :, 32:64], func=AF.Exp),
         reads=[RB.res], writes=[aneg.res])
    K.op("dve", lambda e: e.tensor_scalar(out=aneg.t[:], in0=aneg.t[:], scalar1=-1.0, scalar2=None, op0=ALU.mult),
         reads=[aneg.res], writes=[aneg.res])

    def mm_group(out_ap, bres, pairs, reads, first=True, last=True):
        def emit(e):
            ins = None
            n = len(pairs)
            for k, (l, r) in enumerate(pairs):
                ins = e.matmul(out_ap, l, r, start=(first and k == 0), stop=(last and k == n - 1))
            return ins
        K.op("pe", emit, reads=reads, writes=[bres])

    def rms_rstd(src, eps, ones, name, scratch_sq, rstd):
        K.op("act", lambda e: e.activation(out=scratch_sq.t[:], in_=src.t[:], func=AF.Square),
             reads=[src.res], writes=[scratch_sq.res])
        pb, pr = bank()
        mm_group(pb, pr, [(ones.t[:], scratch_sq.t[:, c, :]) for c in range(DC)], [ones.res, scratch_sq.res])
        K.op("act", lambda e: e.activation(out=rstd.t[:], in_=pb, func=AF.Ln, bias=eps.t[:]),
             reads=[pr, eps.res], writes=[rstd.res])
        K.op("act", lambda e: e.activation(out=rstd.t[:], in_=rstd.t[:], func=AF.Exp, scale=-0.5),
             reads=[rstd.res], writes=[rstd.res])

    hnc = [Res() for _ in range(DC)]
    hc = [Res() for _ in range(DC)]
    nsq = Buf(sb("nsq", [128, DC, T], BF16))
    nsqc = [Res() for _ in range(DC)]
    nrstd = Buf(sb("nrstd", [128, T], F32))
    lndummy = Buf(sb("lndummy", [128, 1], F32))

    def rms_h():
        K.op("act", lambda e: e.activation(out=lndummy.t[:], in_=one_c.t[:], func=AF.Ln),
             reads=[one_c.res], writes=[lndummy.res])
        pb, pr = bank()
        for c in range(DC):
            K.op("act", lambda e, c=c: e.activation(out=nsq.t[:, c, :], in_=h.t[:, c, :], func=AF.Square),
                 reads=[hc[c]], writes=[nsqc[c]])
            K.op("pe", lambda e, c=c: e.matmul(pb, onesD.t[:], nsq.t[:, c, :], start=(c == 0), stop=(c == DC - 1)),
                 reads=[onesD.res, nsqc[c]], writes=[pr])
        K.op("act", lambda e: e.activation(out=nrstd.t[:], in_=pb, func=AF.Ln, bias=eps_rms.t[:]),
             reads=[pr, eps_rms.res], writes=[nrstd.res])
        K.op("act", lambda e: e.activation(out=nrstd.t[:], in_=nrstd.t[:], func=AF.Exp, scale=-0.5),
             reads=[nrstd.res], writes=[nrstd.res])
        return nrstd

    def normalize(src, rstd, wcol, dst):
        assert dst is hn and src is h
        for c in range(DC):
            K.op("dve", lambda e, c=c: e.scalar_tensor_tensor(
                out=dst.t[:, c, :], in0=src.t[:, c, :], scalar=PP.t[:, wcol + c:wcol + c + 1], in1=rstd.t[:],
                op0=ALU.mult, op1=ALU.mult), reads=[hc[c], PP.res, rstd.res], writes=[dst.res, hnc[c]])

    def mm_kc_outer(w, wres, njj):
        outs = [bank() for _ in range(njj)]
        for kc in range(DC):
            def emit(e, kc=kc):
                ins = None
                for jj in range(njj):
                    ins = e.matmul(outs[jj][0], w[:, kc, jj * 128:(jj + 1) * 128], hn.t[:, kc, :],
                                   start=(kc == 0), stop=(kc == DC - 1))
                return ins
            K.op("pe", emit, reads=[wres, hnc[kc]], writes=[o[1] for o in outs])
        return outs

    def ffn_block(i, mid_hook=None):
        with ExitStack() as es:
            es.callback(K.end_scope)
            sq_t = es.enter_context(nc.sbuf_tensor(un("f_sq"), [128, DC, T], BF16))
            rstd_t = es.enter_context(nc.sbuf_tensor(un("f_rstd"), [128, T], F32))
            act_t = es.enter_context(nc.sbuf_tensor(un("f_act"), [128, FC, T], BF16))
            sg0 = es.enter_context(nc.sbuf_tensor(un("f_sg0"), [128, T], F32))
            sg1 = es.enter_context(nc.sbuf_tensor(un("f_sg1"), [128, T], F32))
            sq = Buf(sq_t)
            rstd = Buf(rstd_t)
            act = [Res() for _ in range(FC)]
            sg = [Buf(sg0), Buf(sg1)]
            normalize(h, rms_h(), i * PP_LAYER + 8, hn)
            if mid_hook is not None:
                mid_hook()
            n = 0
            for b in range(6):
                ncol = 512 if b < 5 else 256
                wg, wgr = w_next(8, ncol)
                wu, wur = w_next(8, ncol)
                pre = mm_kc_outer(wg, wgr, 4) if b == 0 else None
                for jj in range(ncol // 128):
                    c = b * 4 + jj
                    if pre is not None:
                        pg, pgr = pre[jj]
                    else:
                        pg, pgr = bank()
                        mm_group(pg, pgr, [(wg[:, kc, jj * 128:(jj + 1) * 128], hn.t[:, kc, :]) for kc in range(DC)],
                                 [wgr, hn.res])
                    pu, pur = bank()
                    mm_group(pu, pur, [(wu[:, kc, jj * 128:(jj + 1) * 128], hn.t[:, kc, :]) for kc in range(DC)],
                             [wur, hn.res])
                    s = sg[n % 2]
                    n += 1
                    K.op("act", lambda e, s=s, pg=pg: e.activation(out=s.t[:], in_=pg, func=AF.Silu),
                         reads=[pgr], writes=[s.res])
                    K.op("dve", lambda e, s=s, pu=pu, c=c: e.tensor_tensor(out=act_t[:, c, :], in0=s.t[:], in1=pu,
                                                                          op=ALU.mult),
                         reads=[s.res, pur], writes=[act[c]])
            for ob in range(8):
                wd, wdr = w_next(FC, 128)
                po, por = bank()
                mm_group(po, por, [(wd[:, kc, :], act_t[:, kc, :]) for kc in range(FC)], [wdr] + act)
                K.op("dve", lambda e, ob=ob, po=po: e.tensor_tensor(out=h.t[:, ob, :], in0=h.t[:, ob, :], in1=po,
                                                                    op=ALU.add),
                     reads=[por, hc[ob]], writes=[hc[ob]])

    def ple_pre(i, t0, eb):
        with ExitStack() as es:
            es.callback(K.end_scope)
            sq = Buf(es.enter_context(nc.sbuf_tensor(un("p_sq"), [128, DC, T], BF16)))
            rstd_e = Buf(es.enter_context(nc.sbuf_tensor(un("p_rstd_e"), [128, T], F32)))
            pbf = Buf(es.enter_context(nc.sbuf_tensor(un("p_pbf"), [128, 2, T], BF16)))
            K.dma("pool", pbf.t[:], pT[i].rearrange("(kc p) s -> p kc s", p=128)[:, :, t0:t0 + T], writes=[pbf.res], scoped=True)
            wp, wpr = w_next(2, 1024)
            for c in range(DC):
                pb, pr = bank()
                mm_group(pb, pr, [(wp[:, kc, c * 128:(c + 1) * 128], pbf.t[:, kc, :]) for kc in range(2)],
                         [wpr, pbf.res])
                K.op("act", lambda e, c=c, pb=pb: e.activation(out=eb.t[:, c, :], in_=pb, func=AF.Copy),
                     reads=[pr], writes=[eb.res])
            rms_rstd(eb, eps_rms, onesD, "ple_e", sq, rstd_e)
            for c in range(DC):
                K.op("dve", lambda e, c=c: e.scalar_tensor_tensor(
                    out=eb.t[:, c, :], in0=eb.t[:, c, :], scalar=PP.t[:, i * PP_LAYER + 16 + c:i * PP_LAYER + 17 + c],
                    in1=rstd_e.t[:], op0=ALU.mult, op1=ALU.mult), reads=[eb.res, PP.res, rstd_e.res],
                    writes=[eb.res])

    def ple_post(i, eb):
        with ExitStack() as es:
            es.callback(K.end_scope)
            sq = Buf(es.enter_context(nc.sbuf_tensor(un("q_sq"), [128, DC, T], BF16)))
            rstd_g = Buf(es.enter_context(nc.sbuf_tensor(un("q_rstd_g"), [128, T], F32)))
            gt = [Buf(es.enter_context(nc.sbuf_tensor(un("q_g%d" % k), [128, T], F32))) for k in range(2)]
            normalize(h, rms_h(), i * PP_LAYER + 24, hn)
            n = 0
            for b in range(2):
                wg, wgr = w_next(8, 512)
                pre = mm_kc_outer(wg, wgr, 4) if b == 0 else None
                for jj in range(4):
                    c = b * 4 + jj
                    if pre is not None:
                        pg, pgr = pre[jj]
                    else:
                        pg, pgr = bank()
                        mm_group(pg, pgr, [(wg[:, kc, jj * 128:(jj + 1) * 128], hn.t[:, kc, :]) for kc in range(DC)],
                                 [wgr, hn.res])
                    g = gt[n % 2]
                    n += 1
                    K.op("act", lambda e, g=g, pg=pg: e.activation(out=g.t[:], in_=pg, func=AF.Sigmoid),
                         reads=[pgr], writes=[g.res])
                    K.op("dve", lambda e, g=g, c=c: e.tensor_tensor(out=g.t[:], in0=g.t[:], in1=eb.t[:, c, :],
                                                                    op=ALU.mult),
                         reads=[g.res, eb.res], writes=[g.res])
                    K.op("dve", lambda e, g=g, c=c: e.tensor_tensor(out=h.t[:, c, :], in0=h.t[:, c, :], in1=g.t[:],
                                                                    op=ALU.add),
                         reads=[g.res, hc[c]], writes=[hc[c]])

    def gmlp_block(i):
        j = i // 2
        with ExitStack() as es:
            es.callback(K.end_scope)
            sq_t = es.enter_context(nc.sbuf_tensor(un("g_sq"), [128, DC, T], BF16))
            rstd_t = es.enter_context(nc.sbuf_tensor(un("g_rstd"), [128, T], F32))
            uu_t = es.enter_context(nc.sbuf_tensor(un("g_uu"), [128, 16, T], BF16))
            vg_t = es.enter_context(nc.sbuf_tensor(un("g_vg"), [128, NCH, 2048], F32))
            vv0 = es.enter_context(nc.sbuf_tensor(un("g_vv0"), [128, 16, 128], BF16))
            vv1 = es.enter_context(nc.sbuf_tensor(un("g_vv1"), [128, 16, 128], BF16))
            gv_t = es.enter_context(nc.sbuf_tensor(un("g_gv"), [128, 3, 2048], F32))
            wsf_t = es.enter_context(nc.sbuf_tensor(un("g_wsf"), [128, 16, 128], F32))
            ws_t = es.enter_context(nc.sbuf_tensor(un("g_ws"), [128, 16, 128], BF16))
            bs_t = es.enter_context(nc.sbuf_tensor(un("g_bs"), [1, 2048], BF16))
            st_t = es.enter_context(nc.sbuf_tensor(un("g_st"), [128, NCH, 4, 6], F32))
            mv_t = es.enter_context(nc.sbuf_tensor(un("g_mv"), [128, NCH, 2], F32))
            rs_t = es.enter_context(nc.sbuf_tensor(un("g_rs"), [128, NCH], F32))
            sq = Buf(sq_t)
            rstd = Buf(rstd_t)
            uu = [[Res() for _ in range(NCH)] for _ in range(16)]
            vg = [Buf(vg_t[:, tc, :]) for tc in range(NCH)]
            vv = [Buf(vv0), Buf(vv1)]
            gvb = Buf(gv_t)
            wsf = Buf(wsf_t)
            wsb = Buf(ws_t)
            bsb = Buf(bs_t)
            stb = [Buf(st_t[:, tc]) for tc in range(NCH)]
            mvb = [Buf(mv_t[:, tc, :]) for tc in range(NCH)]
            rsb = [Buf(rs_t[:, tc:tc + 1]) for tc in range(NCH)]
            K.dma("sp", gvb.t[:].rearrange("p a n -> p (a n)"), gvd[j:j + 1, :].partition_broadcast(128),
                  writes=[gvb.res], scoped=True)
            K.dma("sp", wsf.t[:].rearrange("p g t -> p (g t)"), wsTd[j], writes=[wsf.res], scoped=True)
            K.dma("pool", bsb.t[:], bsd[j:j + 1, :], writes=[bsb.res], scoped=True)
            K.op("dve", lambda e: e.tensor_tensor(out=wsb.t[:], in0=wsf.t[:],
                                                  in1=trif.t[:].unsqueeze(1).to_broadcast([128, 16, 128]),
                                                  op=ALU.mult), reads=[wsf.res, trif.res], writes=[wsb.res])
            normalize(h, rms_h(), i * PP_LAYER + 0, hn)
            bu = PP_GM + j * PP_GM_SZ
            for b in range(4):
                wu, wur = w_next(8, 512)
                pre = mm_kc_outer(wu, wur, 4) if b == 0 else None
                for jj in range(4):
                    c = b * 4 + jj
                    if pre is not None:
                        pb, pr = pre[jj]
                    else:
                        pb, pr = bank()
                        mm_group(pb, pr, [(wu[:, kc, jj * 128:(jj + 1) * 128], hn.t[:, kc, :]) for kc in range(DC)],
                                 [wur, hn.res])
                    K.op("act", lambda e, c=c, pb=pb: e.activation(out=uu_t[:, c, :], in_=pb, func=AF.Gelu,
                                                                   bias=PP.t[:, bu + c:bu + c + 1]),
                         reads=[pr, PP.res], writes=uu[c])
            pend_bn = None
            for b in range(4):
                wv, wvr = w_next(8, 512)
                for tc in range(NCH):
                    pb, pr = bank()
                    mm_group(pb, pr, [(hn.t[:, kc, tc * 128:(tc + 1) * 128], wv[:, kc, :]) for kc in range(DC)],
                             [wvr, hn.res])
                    dst = vg[tc].t[:, b * 512:(b + 1) * 512]
                    K.op("dve", lambda e, dst=dst, pb=pb, b=b: e.tensor_tensor(
                        out=dst, in0=pb, in1=gvb.t[:, 0, b * 512:(b + 1) * 512], op=ALU.add),
                        reads=[pr, gvb.res], writes=[vg[tc].res])
                    K.op("act", lambda e, dst=dst: e.activation(out=dst, in_=dst, func=AF.Gelu),
                         reads=[vg[tc].res], writes=[vg[tc].res])
                    if pend_bn is not None:
                        pend_bn()

                    def pend_bn(dst=dst, tc=tc, b=b):
                        K.op("dve", lambda e: e.bn_stats(out=stb[tc].t[:, b, :], in_=dst),
                             reads=[vg[tc].res], writes=[stb[tc].res])
            pend_bn()
            pend_gate = None
            for tc in range(NCH):
                K.op("dve", lambda e, tc=tc: e.bn_aggr(out=mvb[tc].t, in_=stb[tc].t.rearrange("p a s -> p (a s)")),
                     reads=[stb[tc].res], writes=[mvb[tc].res])
                K.op("act", lambda e, tc=tc: e.activation(out=rsb[tc].t, in_=mvb[tc].t[:, 1:2], func=AF.Ln,
                                                          bias=eps_ln.t[:]),
                     reads=[mvb[tc].res, eps_ln.res], writes=[rsb[tc].res])
                K.op("act", lambda e, tc=tc: e.activation(out=rsb[tc].t, in_=rsb[tc].t, func=AF.Exp, scale=-0.5),
                     reads=[rsb[tc].res], writes=[rsb[tc].res])
                v = vv[tc % 2]
                K.op("dve", lambda e, tc=tc: e.scalar_tensor_tensor(
                    out=vg[tc].t, in0=vg[tc].t, scalar=mvb[tc].t[:, 0:1], in1=gvb.t[:, 1, :],
                    op0=ALU.subtract, op1=ALU.mult), reads=[vg[tc].res, mvb[tc].res, gvb.res], writes=[vg[tc].res])
                K.op("dve", lambda e, tc=tc, v=v: e.scalar_tensor_tensor(
                    out=v.t[:].rearrange("p g d -> p (g d)"), in0=vg[tc].t, scalar=rsb[tc].t, in1=gvb.t[:, 2, :],
                    op0=ALU.mult, op1=ALU.add), reads=[vg[tc].res, rsb[tc].res, gvb.res], writes=[v.res])
                if pend_gate is not None:
                    pend_gate()
                mixb = []
                for q in range(4):
                    pb, pr = bank()
                    mixb.append((pb, pr))
                    for gg in range(4):
                        g = q * 4 + gg

                        def emit(e, g=g, gg=gg, pb=pb, v=v):
                            o = pb[:, gg * 128:(gg + 1) * 128]
                            e.matmul(o, v.t[:, g, :], wsb.t[:, g, :], start=True, stop=False)
                            return e.matmul(o, ones1.t[0:1, :], bsb.t[0:1, g * 128:(g + 1) * 128], start=False,
                                            stop=True)
                        K.op("pe", emit, reads=[v.res, wsb.res, ones1.res, bsb.res], writes=[pr])

                def pend_gate(tc=tc, mixb=mixb):
                    for q in range(4):
                        pb, pr = mixb[q]
                        K.op("dve", lambda e, q=q, pb=pb: e.tensor_tensor(
                            out=uu_t[:, q * 4:(q + 1) * 4, tc * 128:(tc + 1) * 128],
                            in0=uu_t[:, q * 4:(q + 1) * 4, tc * 128:(tc + 1) * 128],
                            in1=pb.rearrange("p (g t) -> p g t", g=4), op=ALU.mult),
                            reads=[pr] + [uu[q * 4 + gg][tc] for gg in range(4)],
                            writes=[uu[q * 4 + gg][tc] for gg in range(4)])
            pend_gate()
            for ob in range(4):
                wo, wor = w_next(16, 256)
                for oc in range(2):
                    c = ob * 2 + oc
                    pb, pr = bank()
                    mm_group(pb, pr, [(wo[:, kc, oc * 128:(oc + 1) * 128], uu_t[:, kc, :]) for kc in range(16)],
                             [wor] + [uu[kc][tc] for kc in range(16) for tc in range(NCH)])
                    K.op("dve", lambda e, c=c, pb=pb: e.tensor_tensor(out=h.t[:, c, :], in0=h.t[:, c, :], in1=pb,
                                                                      op=ALU.add),
                         reads=[pr, hc[c]], writes=[hc[c]])

    def ssd_block(i):
        j = i // 2
        st = state[i]
        hl = halo[i]
        pbase = PP_SSD + j * PP_SSD_SZ
        cw0, cb0, nw0, dc0 = pbase, pbase + 128, pbase + 160, pbase + 176
        with ExitStack() as es:
            es.callback(K.end_scope)

            def al(name, shape, dt):
                return Buf(es.enter_context(nc.sbuf_tensor(un(name), list(shape), dt)))

            normalize(h, rms_h(), i * PP_LAYER + 0, hn)
            zs_b = al("s_zs", [128, 16, T], BF16)
            xbc_b = al("s_xbc", [128, 32, T], BF16)
            zs_t, xbc_t = zs_b.t, xbc_b.t
            NSTG = 3
            stg = [al("s_stg%d" % k, [128, T + 4], F32) for k in range(NSTG)]
            acc = [al("s_acc%d" % k, [128, T], F32) for k in range(NSTG)]
            dtb = al("s_dt", [128, NCH, 32], F32)
            dab = al("s_da", [128, NCH, 32], F32)
            dahl = al("s_dahl", [128, NCH, 2, 32], BF16)
            dmat = al("s_dmat", [128, 16, 128], BF16)
            xr = [al("s_xr%d" % k, [128, 32, 64], BF16) for k in range(2)]
            xrd = [al("s_xrd%d" % k, [128, 32, 64], BF16) for k in range(2)]
            btm = [al("s_btm%d" % k, [128, 8, 128], BF16) for k in range(2)]
            dsx = [al("s_dsx%d" % k, [128, 64], F32) for k in range(2)]
            stbf = [al("s_stbf%d" % k, [128, 32, 64], BF16) for k in range(2)]
            NG = NG_SETS
            cbm = [al("s_cbm%d" % k, [128, 128], BF16) for k in range(NG)]
            rs = [al("s_rs%d" % k, [128, 2, 4, 128], BF16) for k in range(NG)]
            lt = [al("s_lt%d" % k, [128, 4, 128], BF16) for k in range(NG)]
            db = [al("s_db%d" % k, [128, 4, 128], BF16) for k in range(NG)]
            wt = [al("s_wt%d" % k, [128, 4, 128], BF16) for k in range(NG)]
            cd = [al("s_cd%d" % k, [128, 4, 128], BF16) for k in range(NG)]
            gb = [al("s_g%d" % k, [128, 2, 128], F32) for k in range(NG)]
            gq = [al("s_gq%d" % k, [128, 2, 128], BF16) for k in range(NG)]
            gr = [al("s_gr%d" % k, [128, 128], F32) for k in range(NG)]
            zs = [Res() for _ in range(16)]
            xres = [[Res() for _ in range(NCH)] for _ in range(32)]
            if not _dbg.get("ssd"):
                _dbg["ssd"] = 1
                print("ssd scope: sbuf bytes remaining", nc.sbuf_bytes_remaining)

            for jc in range(16):
                K.op("dve", lambda e, jc=jc: e.tensor_scalar(out=dmat.t[:, jc, :], in0=ident.t[:],
                                                             scalar1=PP.t[:, dc0 + jc:dc0 + jc + 1], scalar2=None,
                                                             op0=ALU.mult),
                     reads=[ident.res, PP.res], writes=[dmat.res])
            K.op("act", lambda e: e.activation(out=stbf[0].t[:], in_=st.t[:], func=AF.Copy),
                 reads=[st.res], writes=[stbf[0].res])
            def z_block(b):
                wz, wzr = w_next(8, 512)
                pre = mm_kc_outer(wz, wzr, 4) if b == 0 else None
                for jj in range(4):
                    c = b * 4 + jj
                    if pre is not None:
                        pb, pr = pre[jj]
                    else:
                        pb, pr = bank()
                        mm_group(pb, pr, [(wz[:, kc, jj * 128:(jj + 1) * 128], hn.t[:, kc, :]) for kc in range(DC)],
                                 [wzr, hn.res])
                    K.op("act", lambda e, c=c, pb=pb: e.activation(out=zs_t[:, c, :], in_=pb, func=AF.Silu),
                         reads=[pr], writes=[zs[c]])
            z_block(0)
            wd, wdr = w_next(8, 32)
            pb, pr = bank()
            for c in range(NCH):
                mm_group(pb[:, c * 32:(c + 1) * 32], pr,
                         [(hn.t[:, kc, c * 128:(c + 1) * 128], wd[:, kc, :]) for kc in range(DC)], [wdr, hn.res])
            K.op("dve", lambda e, pb=pb: e.tensor_tensor(
                out=dtb.t[:], in0=pb[:, 0:NCH * 32].rearrange("p (c h) -> p c h", c=NCH),
                in1=RB.t[:, j * 64:j * 64 + 32].unsqueeze(1).to_broadcast([128, NCH, 32]), op=ALU.add),
                reads=[pr, RB.res], writes=[dtb.res])
            K.op("act", lambda e: e.activation(out=dtb.t[:], in_=dtb.t[:], func=AF.Exp),
                 reads=[dtb.res], writes=[dtb.res])
            K.op("act", lambda e: e.activation(out=dtb.t[:], in_=dtb.t[:], func=AF.Ln, bias=one_c.t[:]),
                 reads=[dtb.res, one_c.res], writes=[dtb.res])
            K.op("dve", lambda e: e.tensor_tensor(
                out=dab.t[:], in0=dtb.t[:], in1=aneg.t[:, j * 32:(j + 1) * 32].unsqueeze(1).to_broadcast([128, NCH, 32]),
                op=ALU.mult), reads=[dtb.res, aneg.res], writes=[dab.res])
            K.op("dve", lambda e: e.tensor_copy(dahl.t[:, :, 0, :], dab.t[:]), reads=[dab.res], writes=[dahl.res])
            K.op("dve", lambda e: e.tensor_tensor(out=dahl.t[:, :, 1, :], in0=dab.t[:], in1=dahl.t[:, :, 0, :],
                                                  op=ALU.subtract), reads=[dab.res, dahl.res], writes=[dahl.res])

            def fixed(k):
                return psum_t[:, k * 512:(k + 1) * 512], banks[k]
            XB = [fixed(0), fixed(1), fixed(2)]
            SB = [fixed(3), fixed(4)]
            CB_ = [fixed(5), fixed(6)]
            MISC = fixed(7)

            def P_x(c, half, bankf):
                tk = slice(c * 128, (c + 1) * 128)
                xrc = xr[c % 2]
                pb, pr = bankf()
                pbb = pb.bitcast(BF16)

                def emit(e):
                    ins = None
                    for jj in range(8):
                        ins = e.transpose(pbb[:, jj * 128:(jj + 1) * 128], xbc_t[:, half * 8 + jj, tk], ident.t[:])
                    return ins
                K.op("pe", emit, reads=[ident.res] + [xres[half * 8 + jj][c] for jj in range(8)], writes=[pr])
                K.op("dve", lambda e: e.tensor_tensor(
                    out=xrc.t[:, half * 16:(half + 1) * 16, :], in0=pbb[:, 0:1024].rearrange("p (h q) -> p h q", h=16),
                    in1=dtb.t[:, c, half * 16:(half + 1) * 16].unsqueeze(2).to_broadcast([128, 16, 64]),
                    op=ALU.mult), reads=[pr, dtb.res], writes=[xrc.res])

            def P_b(c, bankf):
                tk = slice(c * 128, (c + 1) * 128)
                btc = btm[c % 2]
                pb, pr = bankf()
                pbb = pb.bitcast(BF16)

                def emit(e):
                    ins = None
                    for g in range(8):
                        ins = e.transpose(pbb[:, g * 128:(g + 1) * 128], xbc_t[:, 16 + g, tk], ident.t[:])
                    return ins
                K.op("pe", emit, reads=[ident.res] + [xres[16 + g][c] for g in range(8)], writes=[pr])
                K.op("act", lambda e: e.activation(out=btc.t[:].rearrange("p g n -> p (g n)"),
                                                   in_=pbb[:, 0:1024], func=AF.Copy),
                     reads=[pr], writes=[btc.res])

            def P_d(c, bankf):
                xrc, xrdc, dsc = xr[c % 2], xrd[c % 2], dsx[c % 2]
                pb, pr = bankf()
                mm_group(pb[:, 0:32], pr, [(sl.t[:], dahl.t[:, c, 0, :]), (sl.t[:], dahl.t[:, c, 1, :])],
                         [sl.res, dahl.res])
                mm_group(pb[:, 32:64], pr, [(ones1.t[:], dahl.t[:, c, 0, :]), (ones1.t[:], dahl.t[:, c, 1, :])],
                         [ones1.res, dahl.res])
                K.op("act", lambda e: e.activation(out=dsc.t[:], in_=pb[:, 0:64], func=AF.Exp),
                     reads=[pr], writes=[dsc.res])

            def P_r(c, halves=(0, 1)):
                xrc, xrdc, dsc = xr[c % 2], xrd[c % 2], dsx[c % 2]
                for hf in halves:
                    K.op(XRD_ENG, lambda e, hf=hf: e.tensor_tensor(
                        out=xrdc.t[:, hf * 16:(hf + 1) * 16, :], in0=xrc.t[:, hf * 16:(hf + 1) * 16, :],
                        in1=dsc.t[:, hf * 16:(hf + 1) * 16].unsqueeze(2).to_broadcast([128, 16, 64]),
                        op=ALU.mult), reads=[xrc.res, dsc.res], writes=[xrdc.res])

            def U_upd(c, gp, bankf):
                xrdc, btc, dsc = xrd[c % 2], btm[c % 2], dsx[c % 2]
                pb, pr = bankf()
                for gg in range(2):
                    g = gp * 2 + gg
                    mm_group(pb[:, gg * 256:(gg + 1) * 256], pr,
                             [(btc.t[:, g, :], xrdc.t[:, 4 * g:4 * g + 4, :].rearrange("p h q -> p (h q)"))],
                             [btc.res, xrdc.res])
                sv = st.t[:, gp * 8:(gp + 1) * 8, :]
                K.op(STM_ENG, lambda e: e.tensor_tensor(
                    out=sv, in0=sv, in1=dsc.t[:, 32 + gp * 8:32 + (gp + 1) * 8].unsqueeze(2).to_broadcast([128, 8, 64]),
                    op=ALU.mult), reads=[st.res, dsc.res], writes=[st.res])
                K.op("dve", lambda e: e.tensor_tensor(
                    out=sv, in0=sv, in1=pb.rearrange("p (h q) -> p h q", h=8), op=ALU.add),
                    reads=[st.res, pr], writes=[st.res])

            def U_copy(c):
                nb = stbf[(c + 1) % 2]
                K.op("act", lambda e: e.activation(out=nb.t[:], in_=st.t[:], func=AF.Copy),
                     reads=[st.res], writes=[nb.res])

            def G1(c, g):
                tk = slice(c * 128, (c + 1) * 128)
                rs_ = rs[g % NG]
                xb, xbr = XB[g % 3]
                K.op(RS_ENG, lambda e: e.tensor_tensor(
                    out=rs_.t[:], in0=tri.t[:].unsqueeze(1).unsqueeze(1).to_broadcast([128, 2, 4, 128]),
                    in1=dahl.t[:, c, :, g * 4:(g + 1) * 4].unsqueeze(3).to_broadcast([128, 2, 4, 128]),
                    op=ALU.mult), reads=[tri.res, dahl.res], writes=[rs_.res])
                mm_group(xb[:, 0:128], xbr, [(xbc_t[:, 16 + g, tk], xbc_t[:, 24 + g, tk])],
                         [xres[16 + g][c], xres[24 + g][c]])
                pseg, psegr = SB[g % 2]
                mm_group(pseg, psegr, [(sl.t[:], rs_.t[:, 0].rearrange("p h l -> p (h l)")),
                                       (sl.t[:], rs_.t[:, 1].rearrange("p h l -> p (h l)"))], [sl.res, rs_.res])
                pcs, pcsr = CB_[g % 2]
                mm_group(pcs, pcsr, [(ones1.t[:], rs_.t[:, 0].rearrange("p h l -> p (h l)")),
                                     (ones1.t[:], rs_.t[:, 1].rearrange("p h l -> p (h l)"))], [ones1.res, rs_.res])

            def G2(c, g):
                tk = slice(c * 128, (c + 1) * 128)
                k3 = g % NG
                xb, xbr = XB[g % 3]
                pseg, psegr = SB[g % 2]
                pcs, pcsr = CB_[g % 2]
                Cfm = xbc_t[:, 24 + g, tk]
                K.op("dve", lambda e: e.tensor_tensor(out=cbm[k3].t[:], in0=xb[:, 0:128], in1=trif.t[:], op=ALU.mult),
                     reads=[xbr, trif.res], writes=[cbm[k3].res])
                K.op("act", lambda e: e.activation(out=lt[k3].t[:].rearrange("p h l -> p (h l)"), in_=pseg, func=AF.Exp),
                     reads=[psegr], writes=[lt[k3].res])
                K.op("act", lambda e: e.activation(out=db[k3].t[:].rearrange("p h l -> p (h l)"), in_=pcs, func=AF.Exp),
                     reads=[pcsr], writes=[db[k3].res])
                K.op("dve", lambda e: e.tensor_tensor(
                    out=wt[k3].t[:], in0=lt[k3].t[:], in1=cbm[k3].t[:].unsqueeze(1).to_broadcast([128, 4, 128]),
                    op=ALU.mult), reads=[lt[k3].res, cbm[k3].res], writes=[wt[k3].res])
                K.op(CD_ENG, lambda e: e.tensor_tensor(
                    out=cd[k3].t[:], in0=db[k3].t[:], in1=Cfm.unsqueeze(1).to_broadcast([128, 4, 128]),
                    op=ALU.mult), reads=[db[k3].res, xres[24 + g][c]], writes=[cd[k3].res])

            def G3(c, g):
                tk = slice(c * 128, (c + 1) * 128)
                k3 = g % NG
                xb, xbr = XB[g % 3]
                xrc = xr[c % 2]
                sbf = stbf[c % 2]
                py = xb[:, 128:384]

                def emit(e):
                    ins = None
                    for jj in range(2):
                        jc = 2 * g + jj
                        o = py[:, jj * 128:(jj + 1) * 128]
                        e.matmul(o, dmat.t[:, jc, :], xbc_t[:, jc, tk], start=True, stop=False)
                        for hh in range(2):
                            hp = 2 * jj + hh
                            hd = 4 * g + hp
                            oh = py[hh * 64:(hh + 1) * 64, jj * 128:(jj + 1) * 128]
                            e.matmul(oh, xrc.t[:, hd, :], wt[k3].t[:, hp, :], start=False, stop=False)
                            ins = e.matmul(oh, sbf.t[:, hd, :], cd[k3].t[:, hp, :], start=False, stop=(hh == 1))
                    return ins
                K.op("pe", emit, reads=[dmat.res, xres[2 * g][c], xres[2 * g + 1][c], xrc.res, wt[k3].res, sbf.res,
                                        cd[k3].res], writes=[xbr])

            def G4a(c, g):
                tk = slice(c * 128, (c + 1) * 128)
                k3 = g % NG
                xb, xbr = XB[g % 3]
                K.op("dve", lambda e: e.tensor_tensor(
                    out=gb[k3].t[:], in0=xb[:, 128:384].rearrange("p (a l) -> p a l", a=2),
                    in1=zs_t[:, 2 * g:2 * g + 2, tk], op=ALU.mult),
                    reads=[xbr, zs[2 * g], zs[2 * g + 1]], writes=[gb[k3].res])
                K.op("act", lambda e: e.activation(out=gq[k3].t[:], in_=gb[k3].t[:], func=AF.Square),
                     reads=[gb[k3].res], writes=[gq[k3].res])

            def G4b1(c, g):
                k3 = g % NG
                xb, xbr = XB[g % 3]
                pss = xb[:, 384:512]
                mm_group(pss, xbr, [(ones256.t[:], gq[k3].t[:, 0, :]), (ones256.t[:], gq[k3].t[:, 1, :])],
                         [ones256.res, gq[k3].res])
                K.op("act", lambda e: e.activation(out=gr[k3].t[:], in_=pss, func=AF.Ln, bias=eps_ln.t[:]),
                     reads=[xbr, eps_ln.res], writes=[gr[k3].res])
                K.op("act", lambda e: e.activation(out=gr[k3].t[:], in_=gr[k3].t[:], func=AF.Exp, scale=-0.5),
                     reads=[gr[k3].res], writes=[gr[k3].res])

            def G4b2(c, g):
                tk = slice(c * 128, (c + 1) * 128)
                k3 = g % NG
                for jj in range(2):
                    jc = 2 * g + jj
                    K.op("dve", lambda e, jj=jj, jc=jc: e.scalar_tensor_tensor(
                        out=xbc_t[:, jc, tk], in0=gb[k3].t[:, jj, :], scalar=PP.t[:, nw0 + jc:nw0 + jc + 1],
                        in1=gr[k3].t[:], op0=ALU.mult, op1=ALU.mult),
                        reads=[gb[k3].res, PP.res, gr[k3].res], writes=[xres[jc][c]])

            def misc_bank():
                return MISC

            P_d(0, bank)

            def hoist(done):
                if done == 7:
                    P_x(0, 0, bank)
                elif done == 15:
                    P_x(0, 1, bank)
                    P_r(0)
                elif done == 23:
                    P_b(0, bank)
                    for gp in range(4):
                        U_upd(0, gp, bank)
            n = 0
            pending = None
            for b in range(8):
                if 1 <= b < 4:
                    z_block(b)
                wx, wxr = w_next(8, 512)
                for jj in range(4):
                    ch = b * 4 + jj
                    pb, pr = bank()
                    mm_group(pb, pr, [(wx[:, kc, jj * 128:(jj + 1) * 128], hn.t[:, kc, :]) for kc in range(DC)],
                             [wxr, hn.res])
                    sg = stg[n % NSTG]
                    ac = acc[n % NSTG]
                    n += 1
                    K.op("act", lambda e, sg=sg, ch=ch: e.activation(out=sg.t[:, 0:3], in_=hl.t[:, ch, :], func=AF.Copy),
                         reads=[hl.res], writes=[sg.res])
                    K.op("act", lambda e, sg=sg, pb=pb: e.activation(out=sg.t[:, 3:3 + T], in_=pb, func=AF.Copy),
                         reads=[pr], writes=[sg.res])
                    K.op("act", lambda e, ac=ac, pb=pb, ch=ch: e.activation(
                        out=ac.t[:], in_=pb, func=AF.Copy, scale=PP.t[:, cw0 + ch * 4 + 3:cw0 + ch * 4 + 4]),
                        reads=[pr, PP.res], writes=[ac.res])
                    K.op("act", lambda e, sg=sg, ch=ch: e.activation(out=hl.t[:, ch, :], in_=sg.t[:, T:T + 3], func=AF.Copy),
                         reads=[sg.res], writes=[hl.res])
                    if pending is not None:
                        pending()
                        hoist(n - 2)
                    for k in range(0, 3):
                        K.op("dve", lambda e, sg=sg, ac=ac, ch=ch, k=k: e.scalar_tensor_tensor(
                            out=ac.t[:], in0=sg.t[:, k:k + T], scalar=PP.t[:, cw0 + ch * 4 + k:cw0 + ch * 4 + k + 1],
                            in1=ac.t[:], op0=ALU.mult, op1=ALU.add), reads=[sg.res, PP.res, ac.res], writes=[ac.res])

                    def pending(ac=ac, ch=ch):
                        K.op("act", lambda e: e.activation(out=xbc_t[:, ch, :], in_=ac.t[:], func=AF.Silu,
                                                           bias=PP.t[:, cb0 + ch:cb0 + ch + 1]),
                             reads=[ac.res, PP.res], writes=xres[ch])
            pending()
            hoist(31)
            NGRP = NCH * 8

            def cg(G):
                return G // 8, G % 8
            for t in range(NGRP + 3):
                if 0 <= t - 3 < NGRP:
                    G4b1(*cg(t - 3))
                if t < NGRP:
                    G1(*cg(t))
                if 0 <= t - 1 < NGRP:
                    G2(*cg(t - 1))
                    G3(*cg(t - 1))
                if 0 <= t - 2 < NGRP:
                    G4a(*cg(t - 2))
                if 0 <= t - 3 < NGRP:
                    G4b2(*cg(t - 3))
                c, r = t // 8, t % 8
                if c < NCH:
                    if r == 0:
                        if c >= 1:
                            U_upd(c, 3, misc_bank)
                        U_copy(c)
                    if c + 1 < NCH:
                        if r == 1:
                            P_d(c + 1, misc_bank)
                        elif r == 2:
                            P_x(c + 1, 0, misc_bank)
                        elif r == 3:
                            P_x(c + 1, 1, misc_bank)
                            P_b(c + 1, misc_bank)
                        elif r == 4:
                            P_r(c + 1, (0,))
                        elif r == 5:
                            P_r(c + 1, (1,))
                            U_upd(c + 1, 0, misc_bank)
                        elif r == 6:
                            U_upd(c + 1, 1, misc_bank)
                        elif r == 7:
                            U_upd(c + 1, 2, misc_bank)
            allx = [xres[kc][tc] for kc in range(16) for tc in range(NCH)]
            for ob in range(4):
                wo, wor = w_next(16, 256)
                for oc in range(2):
                    cc = ob * 2 + oc
                    pb, pr = bank()
                    mm_group(pb, pr, [(wo[:, kc, oc * 128:(oc + 1) * 128], xbc_t[:, kc, :]) for kc in range(16)],
                             [wor] + allx)
                    K.op("dve", lambda e, cc=cc, pb=pb: e.tensor_tensor(out=h.t[:, cc, :], in0=h.t[:, cc, :], in1=pb,
                                                                        op=ALU.add),
                         reads=[pr, hc[cc]], writes=[hc[cc]])

    out_events = []
    for tl in range(NT):
        t0 = tl * T
        for c in range(DC):
            K.dma("sp", h.t[:, c, :], xT[c * 128:(c + 1) * 128, t0:t0 + T], writes=[hc[c]])
        for i in layers:
            if "mix" in parts:
                if i % 2 == 0:
                    ssd_block(i)
                else:
                    gmlp_block(i)
            with ExitStack() as es_l:
                es_l.callback(K.end_scope)
                eb = None
                if "ple" in parts:
                    eb = Buf(es_l.enter_context(nc.sbuf_tensor(un("l_e"), [128, DC, T], F32)))
                if "ffn" in parts:
                    ffn_block(i, (lambda i=i, t0=t0, eb=eb: ple_pre(i, t0, eb)) if eb is not None else None)
                elif eb is not None:
                    ple_pre(i, t0, eb)
                if eb is not None:
                    ple_post(i, eb)
        if do_final:
            with ExitStack() as es:
                es.callback(K.end_scope)
                sq_t = es.enter_context(nc.sbuf_tensor(un("o_sq"), [128, DC, T], BF16))
                rstd_t = es.enter_context(nc.sbuf_tensor(un("o_rstd"), [128, T], F32))
                o_t = es.enter_context(nc.sbuf_tensor(un("o_out"), [128, DC, T], F32))
                sq = Buf(sq_t)
                rstd = Buf(rstd_t)
                ob = Buf(o_t)
                rstd = rms_h()
                for c in range(DC):
                    K.op("dve", lambda e, c=c: e.scalar_tensor_tensor(
                        out=ob.t[:, c, :], in0=h.t[:, c, :], scalar=PP.t[:, PP_FINAL + c:PP_FINAL + c + 1],
                        in1=rstd.t[:], op0=ALU.mult, op1=ALU.mult), reads=[hc[c], PP.res, rstd.res], writes=[ob.res])
                last = K.dma("sp", outT.rearrange("(c p) s -> p c s", p=128)[:, :, t0:t0 + T], ob.t[:],
                             reads=[ob.res], scoped=True)
                out_events.append(last)
        else:
            last = K.dma("sp", outT.rearrange("(c p) s -> p c s", p=128)[:, :, t0:t0 + T], h.t[:], reads=hc)
            out_events.append(last)
    for ev in out_events:
        K.seen["sp"].pop(ev[0].num, None)
        K._wait("sp", ev)
    print("program: %d instructions, %d waits" % (K.nins, K.nwait))
    return nc


def _cols(v):
    v = np.asarray(v, np.float32)
    return np.ascontiguousarray(v.reshape(-1, 128).T)


def pack_params(inp):
    PPh = np.zeros((128, NPP), np.float32)
    for i in range(DEPTH):
        b = i * PP_LAYER
        PPh[:, b:b + 8] = _cols(inp["norm_mix"][i])
        PPh[:, b + 8:b + 16] = _cols(inp["norm_ffn"][i])
        PPh[:, b + 16:b + 24] = _cols(inp["ple_norm"][i])
        PPh[:, b + 24:b + 32] = _cols(inp["ple_gate_norm"][i])
    PPh[:, PP_FINAL:PP_FINAL + 8] = _cols(inp["final_norm"])
    for j in range(2):
        b = PP_SSD + j * PP_SSD_SZ
        cw = np.asarray(inp["ssd_conv_w"][j], np.float32)
        PPh[:, b:b + 128] = cw.reshape(4, 32, 128).transpose(2, 1, 0).reshape(128, 128)
        PPh[:, b + 128:b + 160] = _cols(inp["ssd_conv_b"][j])
        PPh[:, b + 160:b + 176] = _cols(inp["ssd_norm_w"][j])
        dfull = np.repeat(np.asarray(inp["ssd_d"][j], np.float32), 64)
        PPh[:, b + 176:b + 192] = _cols(dfull)
        b2 = PP_GM + j * PP_GM_SZ
        PPh[:, b2:b2 + 16] = _cols(np.asarray(inp["gmlp_b_in"][j])[:2048])
    RBh = np.zeros((128, NRB), np.float32)
    for j in range(2):
        RBh[:, j * 64:j * 64 + 32] = np.asarray(inp["ssd_dt_bias"][j], np.float32)[None, :]
        RBh[:, j * 64 + 32:j * 64 + 64] = np.asarray(inp["ssd_a_log"][j], np.float32)[None, :]
    CSTh = np.zeros((128, NCST), np.float32)
    CSTh[:, 0:128] = np.eye(128, dtype=np.float32)
    k = np.arange(128)
    CSTh[:, 128:256] = (k[:, None] <= k[None, :]).astype(np.float32)
    CSTh[:, 256:384] = (k[:, None] > k[None, :]).astype(np.float32)
    gv = np.zeros((2, 3 * 2048), np.float32)
    wsT = np.zeros((2, 128, 2048), np.float32)
    bs = np.zeros((2, 2048), np.float32)
    for j in range(2):
        gv[j, 0:2048] = np.asarray(inp["gmlp_b_in"][j])[2048:]
        gv[j, 2048:4096] = inp["gmlp_ln_w"][j]
        gv[j, 4096:6144] = inp["gmlp_ln_b"][j]
        wsT[j] = np.asarray(inp["gmlp_w_s"][j], np.float32).transpose(2, 0, 1).reshape(128, 2048)
        bs[j] = np.asarray(inp["gmlp_b_s"][j], np.float32).reshape(2048)
    return {"PP": PPh, "RB": RBh, "CST": CSTh, "gv": gv, "wsT": wsT, "bs": bs}


_WNAMES = ["ssd_w_in", "ssd_w_out", "gmlp_w_in", "gmlp_w_out", "ffn_w_gate", "ffn_w_up", "ffn_w_down",
           "ple_w_proj", "ple_w_gate"]


def make_in_maps(inp, n_cores, S_tok):
    shared = pack_params(inp)
    for n in _WNAMES:
        shared[n] = np.ascontiguousarray(np.asarray(inp[n], np.float32))
    x = np.asarray(inp["x"], np.float32)
    p = np.asarray(inp["p"], np.float32)
    maps = []
    for b in range(n_cores):
        m = dict(shared)
        m["xT"] = np.ascontiguousarray(x[b, :S_tok].T)
        m["pT"] = np.ascontiguousarray(p[:, b, :S_tok].transpose(0, 2, 1))
        maps.append(m)
    return maps


def kernel(**inputs):
    x = np.asarray(inputs["x"])
    B, S_tok, _ = x.shape
    nc = build_program(S_tok)
    maps = make_in_maps(inputs, B, S_tok)
    res = run_bass_kernel_spmd(nc, maps, core_ids=list(range(B)))
    out = np.stack([np.ascontiguousarray(res.results[b]["outT"].T) for b in range(B)], axis=0)
    return out.astype(np.float32)
```
